# Optimizing a Trainium2 kernel written in Bass

```python
import functools
import jax, jax.numpy as jnp
from jax import lax
import numpy as np

D_MODEL = 4096
BATCH = 2
SEQ = 4096
DEPTH = 1
DEC_BATCH = 128
DEC_SEQ = 8
PAST_LEN = 8192
PAGE_SIZE = 128

ATTN_WIDTH = D_MODEL // 2
D_CONV = D_MODEL - ATTN_WIDTH
HEAD_DIM = 64
N_Q_HEADS = ATTN_WIDTH // HEAD_DIM
N_KV_HEADS = max(1, N_Q_HEADS // 8)
GROUP = N_Q_HEADS // N_KV_HEADS
KV_DIM = N_KV_HEADS * HEAD_DIM
WINDOW = 128
BLOCK = 128
CONV_K = 3
D_FF = ((8 * D_MODEL // 3 + 255) // 256) * 256
N_IN = ATTN_WIDTH + 2 * KV_DIM + 3 * D_CONV
N_MOD = 6
EPS = 1e-5

kernel_name = "hybrid_swa_sink_shortconv_convffn_adaln_step"


def _rmsnorm(x, g):
    xf = x.astype(jnp.float32)
    y = xf * lax.rsqrt(jnp.mean(xf * xf, axis=-1, keepdims=True) + EPS)
    return (y * g.astype(jnp.float32)).astype(x.dtype)


def _causal_dwconv(u, prev, w):
    t = u.shape[1]
    full = jnp.concatenate([prev.astype(u.dtype), u], axis=1)
    y = full[:, 0:t] * w[0]
    for j in range(1, CONV_K):
        y = y + full[:, j:j + t] * w[j]
    return y, full[:, full.shape[1] - (CONV_K - 1):]


def _sink_softmax(s, mask, sinks):
    sink = sinks.astype(jnp.float32).reshape(N_KV_HEADS, GROUP, 1, 1)
    s = jnp.where(mask, s, -jnp.inf)
    m = jnp.maximum(jnp.max(s, axis=-1, keepdims=True), sink)
    p = jnp.exp(s - m)
    return p / (jnp.sum(p, axis=-1, keepdims=True) + jnp.exp(sink - m))


def _prompt_attention(q, k, v, sinks):
    n, t = q.shape[0], q.shape[1]
    nb = t // BLOCK
    qb = q.reshape(n, nb, BLOCK, N_KV_HEADS, GROUP, HEAD_DIM)
    kb = k.reshape(n, nb, BLOCK, N_KV_HEADS, HEAD_DIM)
    vb = v.reshape(n, nb, BLOCK, N_KV_HEADS, HEAD_DIM)

    def with_prev(a):
        prev = jnp.concatenate([jnp.zeros_like(a[:, :1]), a[:, :-1]], axis=1)
        return jnp.concatenate([prev, a], axis=2)

    kk, vv = with_prev(kb), with_prev(vb)
    s = jnp.einsum("nbqkgd,nbskd->nbkgqs", qb, kk).astype(jnp.float32) * (HEAD_DIM ** -0.5)
    blk = jnp.arange(nb)[:, None, None] * BLOCK
    qpos = blk + jnp.arange(BLOCK)[None, :, None]
    kpos = blk - BLOCK + jnp.arange(2 * BLOCK)[None, None, :]
    diff = qpos - kpos
    mask = (diff >= 0) & (diff <= WINDOW) & (kpos >= 0)
    p = _sink_softmax(s, mask[:, None, None], sinks).astype(v.dtype)
    o = jnp.einsum("nbkgqs,nbskd->nbqkgd", p, vv).reshape(n, t, N_Q_HEADS * HEAD_DIM)
    keep = min(WINDOW, t)
    return o, k[:, t - keep:], v[:, t - keep:]


def _sample_attention(q, k, v, k_buf, v_buf, sinks):
    n, t = q.shape[0], q.shape[1]
    w = k_buf.shape[1]
    kk = jnp.concatenate([k_buf.astype(k.dtype), k], axis=1)
    vv = jnp.concatenate([v_buf.astype(v.dtype), v], axis=1)
    qg = q.reshape(n, t, N_KV_HEADS, GROUP, HEAD_DIM)
    s = jnp.einsum("nqkgd,nskd->nkgqs", qg, kk).astype(jnp.float32) * (HEAD_DIM ** -0.5)
    qpos = PAST_LEN + jnp.arange(t)
    kpos = jnp.concatenate([PAST_LEN - w + jnp.arange(w), PAST_LEN + jnp.arange(t)])
    diff = qpos[:, None] - kpos[None, :]
    mask = (diff >= 0) & (diff <= WINDOW)
    p = _sink_softmax(s, mask, sinks).astype(v.dtype)
    o = jnp.einsum("nkgqs,nskd->nqkgd", p, vv).reshape(n, t, N_Q_HEADS * HEAD_DIM)
    return o, kk[:, kk.shape[1] - w:], vv[:, vv.shape[1] - w:]


def _layer(x, c, conv_prev, ffn_prev, attend, w_mod, b_mod, g_mix, g_ffn, w_in, conv_w,
           w_out, w_gate_up, ffn_conv_w, w_down):
    n, t, _ = x.shape
    mod = (jax.nn.silu(c) @ w_mod + b_mod).reshape(n, N_MOD, 1, D_MODEL)
    shift_m, scale_m, gate_m = mod[:, 0], mod[:, 1], mod[:, 2]
    shift_f, scale_f, gate_f = mod[:, 3], mod[:, 4], mod[:, 5]

    h = _rmsnorm(x, g_mix) * (1 + scale_m) + shift_m
    proj = h @ w_in
    cuts = [ATTN_WIDTH, ATTN_WIDTH + KV_DIM, ATTN_WIDTH + 2 * KV_DIM,
            ATTN_WIDTH + 2 * KV_DIM + D_CONV, ATTN_WIDTH + 2 * KV_DIM + 2 * D_CONV]
    q, k, v, b_gate, c_gate, h_conv = jnp.split(proj, cuts, axis=-1)
    attn_out, k_state, v_state = attend(
        q.reshape(n, t, N_Q_HEADS, HEAD_DIM),
        k.reshape(n, t, N_KV_HEADS, HEAD_DIM),
        v.reshape(n, t, N_KV_HEADS, HEAD_DIM))
    y_conv, conv_state = _causal_dwconv(c_gate * h_conv, conv_prev, conv_w)
    mixed = jnp.concatenate([attn_out, b_gate * y_conv], axis=-1) @ w_out
    x = x + gate_m * mixed

    h = _rmsnorm(x, g_ffn) * (1 + scale_f) + shift_f
    gate_pre, up = jnp.split(h @ w_gate_up, [D_FF], axis=-1)
    g_conv, ffn_state = _causal_dwconv(gate_pre, ffn_prev, ffn_conv_w)
    x = x + gate_f * ((jax.nn.silu(g_conv) * up) @ w_down)
    return x, k_state, v_state, conv_state, ffn_state


def setup_inputs(seed: int = 0) -> dict:
    key = jax.random.key(seed)
    ks = jax.random.split(key, 20)
    f32 = jnp.float32

    def nrm(k, shape, scale):
        return jax.random.normal(k, shape, f32) * scale

    wb = min(WINDOW, PAST_LEN)
    return {
        "x_prompt": nrm(ks[0], (BATCH, SEQ, D_MODEL), 1.0),
        "x_sample": nrm(ks[1], (DEC_BATCH, DEC_SEQ, D_MODEL), 1.0),
        "cache_k_win": nrm(ks[2], (DEPTH, DEC_BATCH, wb, N_KV_HEADS, HEAD_DIM), 1.0),
        "cache_v_win": nrm(ks[3], (DEPTH, DEC_BATCH, wb, N_KV_HEADS, HEAD_DIM), 1.0),
        "state_conv": nrm(ks[4], (DEPTH, DEC_BATCH, CONV_K - 1, D_CONV), 1.0),
        "state_ffn_conv": nrm(ks[5], (DEPTH, DEC_BATCH, CONV_K - 1, D_FF), 1.0),
        "c_prompt": nrm(ks[6], (BATCH, D_MODEL), 1.0),
        "c_sample": nrm(ks[7], (DEC_BATCH, D_MODEL), 1.0),
        "w_mod": nrm(ks[8], (DEPTH, D_MODEL, N_MOD * D_MODEL), 0.5 * D_MODEL ** -0.5),
        "b_mod": nrm(ks[9], (DEPTH, N_MOD * D_MODEL), 0.02),
        "g_mix": 1.0 + nrm(ks[10], (DEPTH, D_MODEL), 0.05),
        "g_ffn": 1.0 + nrm(ks[11], (DEPTH, D_MODEL), 0.05),
        "w_in": nrm(ks[12], (DEPTH, D_MODEL, N_IN), D_MODEL ** -0.5),
        "conv_w": nrm(ks[13], (DEPTH, CONV_K, D_CONV), CONV_K ** -0.5),
        "sinks": nrm(ks[14], (DEPTH, N_Q_HEADS), 1.0),
        "w_out": nrm(ks[15], (DEPTH, ATTN_WIDTH + D_CONV, D_MODEL), (ATTN_WIDTH + D_CONV) ** -0.5),
        "w_gate_up": nrm(ks[16], (DEPTH, D_MODEL, 2 * D_FF), D_MODEL ** -0.5),
        "ffn_conv_w": nrm(ks[17], (DEPTH, CONV_K, D_FF), CONV_K ** -0.5),
        "w_down": nrm(ks[18], (DEPTH, D_FF, D_MODEL), D_FF ** -0.5),
        "g_final": 1.0 + nrm(ks[19], (D_MODEL,), 0.05),
    }


def reference(x_prompt, x_sample, cache_k_win, cache_v_win, state_conv, state_ffn_conv,
              c_prompt, c_sample, w_mod, b_mod, g_mix, g_ffn, w_in, conv_w, sinks, w_out,
              w_gate_up, ffn_conv_w, w_down, g_final):
    xp, xs = x_prompt, x_sample
    kp_l, vp_l, cp_l, fp_l = [], [], [], []
    ks_l, vs_l, cs_l, fs_l = [], [], [], []
    for l in range(DEPTH):
        weights = (w_mod[l], b_mod[l], g_mix[l], g_ffn[l], w_in[l], conv_w[l], w_out[l],
                   w_gate_up[l], ffn_conv_w[l], w_down[l])
        conv0 = jnp.zeros((xp.shape[0], CONV_K - 1, D_CONV), xp.dtype)
        ffn0 = jnp.zeros((xp.shape[0], CONV_K - 1, D_FF), xp.dtype)
        xp, kp, vp, cp, fp = _layer(
            xp, c_prompt, conv0, ffn0,
            functools.partial(_prompt_attention, sinks=sinks[l]), *weights)
        xs, ksn, vsn, csn, fsn = _layer(
            xs, c_sample, state_conv[l], state_ffn_conv[l],
            functools.partial(_sample_attention, k_buf=cache_k_win[l], v_buf=cache_v_win[l],
                              sinks=sinks[l]), *weights)
        kp_l.append(kp); vp_l.append(vp); cp_l.append(cp); fp_l.append(fp)
        ks_l.append(ksn); vs_l.append(vsn); cs_l.append(csn); fs_l.append(fsn)
    y_prompt = _rmsnorm(xp, g_final)
    y_sample = _rmsnorm(xs, g_final)
    return (y_prompt, y_sample,
            jnp.stack(kp_l), jnp.stack(vp_l), jnp.stack(cp_l), jnp.stack(fp_l),
            jnp.stack(ks_l), jnp.stack(vs_l), jnp.stack(cs_l), jnp.stack(fs_l))
```

```python
import numpy as np
from contextlib import ExitStack
import concourse.bass as bass
import concourse.mybir as mybir
from concourse.bass_utils import run_bass_kernel_spmd

F32 = mybir.dt.float32
BF = mybir.dt.bfloat16
AF = mybir.ActivationFunctionType
OP = mybir.AluOpType

D = 4096
KC = 32
DFF = 11008
FC = 86
NIN = 8704
EPS = 1e-5
ENGS = ("pe", "act", "dve", "pool", "sp")
import os
K_STOP = int(os.environ.get("K_STOP", "99"))


class _StopBuild(Exception):
    pass


class Res:
    def __init__(self, name=""):
        self.name = name
        self.w = None
        self.r = {}


class Slot:
    def __init__(self, sem):
        self.sem = sem
        self.count = 0


class _Rec:
    def __init__(self):
        self.call = None

    def __getattr__(self, name):
        def f(*a, **k):
            self.call = (name, a, k)
            return self
        return f


class Sched:
    def __init__(self):
        self.streams = {k: [] for k in ENGS}
        self.count = {k: 0 for k in ENGS}
        self.sems = {}
        self.waited = {k: {} for k in ENGS}

    def _need(self, eng, ev):
        if ev is None:
            return
        sem, val, src = ev
        if src == eng and eng == "pe":
            return
        key = id(sem)
        if self.waited[eng].get(key, 0) >= val:
            return
        self.waited[eng][key] = val
        self.streams[eng].append(lambda e, sem=sem, val=val: e.wait_ge(sem, val))

    def _deps(self, eng, reads, writes):
        for r in reads:
            self._need(eng, r.w)
        for w in writes:
            self._need(eng, w.w)
            for ev in list(w.r.values()):
                self._need(eng, ev)

    def _commit(self, ev, reads, writes):
        for r in reads:
            r.r[id(ev[0])] = ev
        for w in writes:
            w.w = ev
            w.r = {}

    def op(self, eng, fn, reads=(), writes=(), inc=True):
        writes = list(writes) + [r for r in reads if r.name.startswith("ps")]
        reads = [r for r in reads if not r.name.startswith("ps")]
        self._deps(eng, reads, writes)
        if not inc:
            rec = _Rec()
            fn(rec)
            name, a, k = rec.call
            self.streams[eng].append(lambda e, name=name, a=a, k=k: getattr(e, name)(*a, **k))
            return
        self.count[eng] += 1
        sem = self.sems[eng]
        ev = (sem, self.count[eng], eng)
        rec = _Rec()
        fn(rec)
        name, a, k = rec.call
        self.streams[eng].append(lambda e, name=name, a=a, k=k, sem=sem: getattr(e, name)(*a, **k).then_inc(sem, 1))
        self._commit(ev, reads, writes)

    def dma(self, q, out, in_, slot, reads=(), writes=(), **kw):
        self._deps(q, reads, writes)
        slot.count += 16
        ev = (slot.sem, slot.count, "dma")
        self.streams[q].append(
            lambda e, out=out, in_=in_, s=slot.sem, kw=kw: e.dma_start(out=out, in_=in_, **kw).then_inc(s, 16))
        self._commit(ev, reads, writes)

    def wait_all(self, eng, slots):
        for s in slots:
            if s.count:
                self._need(eng, (s.sem, s.count, "dma"))


def build_program():
    nc = bass.Bass("TRN2", target_bir_lowering=False)

    def din(name, shape):
        return nc.dram_tensor(name, list(shape), F32, kind="ExternalInput").ap()

    def dout(name, shape):
        return nc.dram_tensor(name, list(shape), F32, kind="ExternalOutput").ap()

    xp = din("xp", [1280, D]); xs = din("xs", [128, D]); c17 = din("c17", [17, D])
    kcT_d = din("kcT", [128, 16, 4, 128]); vc_d = din("vc", [16, 128, 256]); kc_d = din("kc", [16, 128, 256])
    ush_d = din("ush", [128, 16, 16, 2]); gph_d = din("gph", [128, FC, 16, 2])
    w_mod = din("w_mod", [D, 6 * D]); b_mod = din("b_mod", [6, D])
    g_mix = din("g_mix", [D]); g_ffn = din("g_ffn", [D]); g_fin = din("g_fin", [D])
    w_in = din("w_in", [D, NIN]); w_out = din("w_out", [D, D]); w_gu = din("w_gu", [D, 2 * DFF]); w_dn = din("w_dn", [DFF, D])
    cw_d = din("cw", [128, 16, 3]); fcw_d = din("fcw", [128, FC, 3]); snk_d = din("snk", [128, 16])
    ident_d = din("ident", [128, 128]); masks_d = din("masks", [128, 5, 512]); sel_d = din("sel", [18, 2, 128])
    flag_d = din("flag", [128, 1])

    y_p = dout("y_p", [1024, D]); y_s = dout("y_s", [128, D])
    kwin_p = dout("kwin_p", [128, 256]); vwin_p = dout("vwin_p", [128, 256])
    conv_p = dout("conv_p", [128, 16, 2]); ffn_p = dout("ffn_p", [128, FC, 2])
    kwin_s = dout("kwin_s", [16, 128, 256]); vwin_s = dout("vwin_s", [16, 128, 256])
    conv_s = dout("conv_s", [128, 16, 16, 2]); ffn_s = dout("ffn_s", [128, FC, 16, 2])

    modb = nc.dram_tensor("modb", [2, 6, 128, D], F32).ap()
    xmid = nc.dram_tensor("xmid", [10, 128, D], F32).ap()

    S = Sched()
    es = ExitStack()
    with es:
        POOLW = 162 * 256
        pool = es.enter_context(nc.sbuf_tensor("pool", [128, POOLW], F32))

        def fv(kb0, nwords, shape=None):
            o = int(kb0 * 256)
            ap = pool[:, o:o + nwords]
            return ap

        def bv(kb0, nelem):
            o = int(kb0 * 256)
            return pool[:, o:o + nelem // 2].bitcast(BF)

        def sbt(name, shape, dtype=F32):
            return es.enter_context(nc.sbuf_tensor("sb_" + name, list(shape), dtype))

        ident = sbt("ident", [128, 128]); masks = sbt("masks", [128, 5, 512], BF)
        sel = sbt("sel", [18, 2, 128]); flag = sbt("flag", [128, 1])
        cw = sbt("cw", [128, 16, 3]); fcw = sbt("fcw", [128, FC, 3]); esk = sbt("esk", [128, 16])
        ucarry = sbt("ucarry", [128, 16, 2]); gcarry = sbt("gcarry", [128, FC, 2])
        ush = sbt("ush", [128, 16, 16, 2]); gq = sbt("gq", [128, 2, 16, 2])
        usl = sbt("usl", [128, 16, 16, 2]); gso = sbt("gso", [128, 2, 16, 2])
        kvo = sbt("kvo", [128, 2, 512]); stats = sbt("stats", [128, 8])
        cT = sbt("cT", [128, KC, 17], BF); onesb = sbt("onesb", [128, 64], BF)
        wk = [sbt("wk%d" % i, [128, 516]) for i in range(6)]
        PS = [es.enter_context(nc.psum_tensor("ps%d" % i, [128, 512], F32)) for i in range(8)]
        for k in ENGS:
            S.sems[k] = es.enter_context(nc.semaphore("sem_" + k))
        slots = {}

        def slot(name):
            if name not in slots:
                slots[name] = Slot(es.enter_context(nc.semaphore("d_" + name)))
            return slots[name]

        R = {}

        PERSIST = ("const", "masks", "ones", "ucarry", "gcarry", "esk", "usl", "gso", "gq", "kvo", "st", "cT", "wk", "ps", "modb", "xmid")
        FEN = {"evs": {}}

        def is_pool(name):
            return not any(name.startswith(p) for p in PERSIST)

        def res(name):
            if name not in R:
                R[name] = Res(name)
                if is_pool(name):
                    for ev in FEN["evs"].values():
                        R[name].r[id(ev[0])] = ev
            return R[name]

        def fence_all(*_a):
            evs = dict(FEN["evs"])
            for n, rr in R.items():
                if not is_pool(n):
                    continue
                cand = list(rr.r.values())
                if rr.w is not None:
                    cand.append(rr.w)
                for ev in cand:
                    k = id(ev[0])
                    if k not in evs or evs[k][1] < ev[1]:
                        evs[k] = ev
            FEN["evs"] = evs
            for n, rr in R.items():
                if not is_pool(n):
                    continue
                for k, ev in evs.items():
                    if k not in rr.r or rr.r[k][1] < ev[1]:
                        rr.r[k] = ev

        def fence(a, b):
            fence_all()

        psr = [res("ps%d" % i) for i in range(8)]
        wkr = [res("wk%d" % i) for i in range(6)]

        def W32(kb):
            return int(kb * 256)

        su = slot("setup")
        r_const = res("const")
        for (t, d) in ((ident, ident_d), (sel, sel_d), (flag, flag_d), (cw, cw_d), (fcw, fcw_d),
                       (ush, ush_d)):
            S.dma("sp", t[:], d, su, writes=[r_const])
        S.dma("sp", esk[:], snk_d, su, writes=[r_const])
        S.dma("pool", masks[:], masks_d, slot("setup2"), writes=[res("masks")])
        S.op("dve", lambda e: e.memset(onesb[:], 1.0), writes=[res("ones")])
        S.op("dve", lambda e: e.memset(ucarry[:], 0.0), writes=[res("ucarry")])
        S.op("dve", lambda e: e.memset(gcarry[:], 0.0), writes=[res("gcarry")])
        S.op("act", lambda e: e.activation(esk[:], esk[:], AF.Exp), reads=[r_const], writes=[res("esk")])

        csl = fv(98, D)
        mrow = fv(114, D)
        gb = fv(130, D)
        mst = [fv(146, D), fv(0, D)]
        slabA = [bv(16, 16 * 512), bv(32, 16 * 512), bv(48, 16 * 512)]
        S.dma("sp", csl[0:17, :], c17, slot("c17"), writes=[res("csl")])
        S.op("act", lambda e: e.activation(csl[0:17, :], csl[0:17, :], AF.Silu), reads=[res("csl")], writes=[res("csl")])
        for q4 in range(8):
            pb = PS[q4 % 2]
            for j in range(4):
                kc = q4 * 4 + j
                S.op("pe", lambda e, kc=kc, j=j, pb=pb: e.transpose(pb[:, j * 17:(j + 1) * 17], csl[0:17, kc * 128:(kc + 1) * 128], ident[0:17, 0:17]),
                     reads=[res("csl"), r_const], writes=[psr[q4 % 2]], inc=(j == 3))
            S.op("dve", lambda e, q4=q4, pb=pb: e.tensor_copy(cT[:, q4 * 4:(q4 + 1) * 4, :], pb[:, 0:68].rearrange("p (a b) -> p a b", a=4)),
                 reads=[psr[q4 % 2]], writes=[res("cT")])
        wmod_v = w_mod.rearrange("(kc p) c -> p kc c", p=128)
        sidx = 0
        for m in range(6):
            S.dma("sp", mrow[17:18, :], b_mod[m:m + 1, :], slot("bmod"), writes=[res("mrow")])
            if m in (1, 4):
                S.dma("sp", gb[:, :], (g_mix if m == 1 else g_ffn)[:].partition_broadcast(128), slot("gb"), writes=[res("gb")])
            for cb in range(8):
                pb = PS[2 + cb % 2]
                for half in range(2):
                    sl = slabA[sidx % 3]; sr = res("slabA%d" % (sidx % 3)); ss = slot("slabA%d" % (sidx % 3)); sidx += 1
                    c0 = m * D + cb * 512
                    S.dma("pool", sl.rearrange("p (k c) -> p k c", k=16), wmod_v[:, half * 16:(half + 1) * 16, c0:c0 + 512], ss, writes=[sr])
                    for k in range(16):
                        kc = half * 16 + k
                        S.op("pe", lambda e, pb=pb, sl=sl, k=k, kc=kc: e.matmul(pb[0:17, :], cT[:, kc, :], sl[:, k * 512:(k + 1) * 512], start=(kc == 0), stop=(kc == 31)),
                             reads=[res("cT"), sr], writes=[psr[2 + cb % 2]], inc=(k == 15))
                S.op("act", lambda e, pb=pb, cb=cb: e.activation(mrow[0:17, cb * 512:(cb + 1) * 512], pb[0:17, :], AF.Copy),
                     reads=[psr[2 + cb % 2]], writes=[res("mrow")])
            for ty in range(2):
                st = mst[ty]; sr = res("mst%d" % ty)
                for cb in range(8):
                    pb = PS[4 + cb % 2]
                    S.op("pe", lambda e, pb=pb, cb=cb, ty=ty: e.matmul(pb[:, :], sel[:, ty, :], mrow[0:18, cb * 512:(cb + 1) * 512], start=True, stop=True),
                         reads=[res("mrow"), r_const], writes=[psr[4 + cb % 2]])
                    if m in (1, 4):
                        S.op("dve", lambda e, pb=pb, cb=cb, st=st: e.scalar_tensor_tensor(st[:, cb * 512:(cb + 1) * 512], pb[:, :], 1.0, gb[:, cb * 512:(cb + 1) * 512], OP.add, OP.mult),
                             reads=[psr[4 + cb % 2], res("gb")], writes=[sr])
                    else:
                        S.op("dve", lambda e, pb=pb, cb=cb, st=st: e.tensor_copy(st[:, cb * 512:(cb + 1) * 512], pb[:, :]),
                             reads=[psr[4 + cb % 2]], writes=[sr])
                S.dma("sp", modb[ty, m], st[:, :], slot("mst%d" % ty), reads=[sr], writes=[res("modb%d_%d" % (ty, m))])
        P0_names = ["csl", "mrow", "gb", "mst0", "mst1", "slabA0", "slabA1", "slabA2"]

        hT = bv(0, 32 * 640).rearrange("p (k c) -> p k c", k=32)
        kT = bv(40, 4 * 640).rearrange("p (g c) -> p g c", g=4)
        Vt = bv(45, 5 * 256).rearrange("p (t c) -> p t c", t=5)
        PT = [bv(48 + i, 512) for i in range(4)]
        rec = [fv(52, 512), fv(54, 512)]
        cwk = [fv(56 + 2 * i, 512) for i in range(5)]
        aT = bv(0, FC * 384).rearrange("p (f c) -> p f c", f=FC)
        qT = bv(66, 16 * 512).rearrange("p (k c) -> p k c", k=16)
        cvT = bv(82, 16 * 512).rearrange("p (k c) -> p k c", k=16)
        zT = bv(66, 32 * 512).rearrange("p (k c) -> p k c", k=32)
        xt = [fv(98, D), fv(114, D)]
        nt1 = {"tq": [fv(130, 1024), fv(134, 1024)], "gs": [fv(138, 1024), fv(142, 1024)], "sh": [fv(146, 1024), fv(150, 1024)]}
        nt3 = {"tq": [fv(0, 1024), fv(4, 1024)], "gs": [fv(8, 1024), fv(12, 1024)], "sh": [fv(16, 1024), fv(20, 1024)]}
        gpc = [fv(24, 512), fv(26, 512)]
        slab2 = [bv(98, 32 * 256), bv(114, 32 * 256), bv(130, 32 * 256)]
        slab3 = [bv(28, 16 * 512), bv(44, 16 * 512)]
        kcT = bv(146, 8 * 4 * 128).rearrange("p (s g k) -> p s g k", s=8, g=4)
        vcs = bv(154, 8 * 256).rearrange("p (s c) -> p s c", s=8)
        xm = fv(98, 4 * D).rearrange("p (t c) -> p t c", t=4)
        xo = fv(98, 3 * D).rearrange("p (t c) -> p t c", t=3)
        slab5 = [bv(146, 8 * 512), bv(154, 8 * 512)]
        ytmp = fv(66, D)
        pc5 = [fv(82, 512), fv(84, 512), fv(86, 512), fv(88, 512)]
        fwk = [fv(146 + 2 * i, 512) for i in range(6)]

        w_in_v = w_in.rearrange("(kc p) c -> p kc c", p=128)
        w_out_v = w_out.rearrange("(kc p) c -> p kc c", p=128)
        w_gu_v = w_gu.rearrange("(kc p) c -> p kc c", p=128)
        w_dn_v = w_dn.rearrange("(fc p) c -> p fc c", p=128)

        A_names = ["hT", "kT", "Vt", "PT0", "PT1", "PT2", "PT3", "rec0", "rec1", "cwk0", "cwk1", "cwk2", "cwk3", "cwk4",
                   "aT", "nt3", "gpc0", "gpc1", "slab3_0", "slab3_1", "mst1", "slabA0", "slabA1", "slabA2"]
        B_names = ["qT", "cvT", "zT", "ytmp", "pc5_0", "pc5_1", "pc5_2", "pc5_3"]
        C_names = ["csl", "mrow", "gb", "mst0", "xt0", "xt1", "nt1", "slab2_0", "slab2_1", "slab2_2", "kcT", "vcs", "xm0", "xm1", "xm2", "xm3",
                   "xo0", "xo1", "xo2", "slab5_0", "slab5_1", "fwk"]

        slab_ctr = {"s2": 0, "s3": 0, "s5": 0}

        def norm_tile(src, src_res, ty, m_scale, m_shift, ntb, dstT, dst_res, col0, tmp_pref):
            st = stats
            S.op("act", lambda e: e.activation(ntb["tq"][0][:, :], src[:, 0:1024], AF.Square, accum_out=st[:, 0:1]),
                 reads=[src_res], writes=[res(tmp_pref + "tq0"), res("st0")])
            for q in range(1, 4):
                S.op("act", lambda e, q=q: e.activation(ntb["tq"][0][:, :], src[:, q * 1024:(q + 1) * 1024], AF.Square, accum_out=st[:, q:q + 1]),
                     reads=[src_res], writes=[res(tmp_pref + "tq0"), res("st%d" % q)])
            S.op("dve", lambda e: e.tensor_reduce(st[:, 4:5], st[:, 0:4], mybir.AxisListType.X, OP.add),
                 reads=[res("st0"), res("st1"), res("st2"), res("st3")], writes=[res("st4")])
            S.op("dve", lambda e: e.tensor_scalar(st[:, 5:6], st[:, 4:5], 1.0 / D, EPS, OP.mult, OP.add), reads=[res("st4")], writes=[res("st5")])
            S.op("act", lambda e: e.activation(st[:, 6:7], st[:, 5:6], AF.Sqrt), reads=[res("st5")], writes=[res("st6")])
            S.op("dve", lambda e: e.reciprocal(st[:, 7:8], st[:, 6:7]), reads=[res("st6")], writes=[res("st7")])
            for q in range(4):
                b = q % 2
                gs = ntb["gs"][b]; sh = ntb["sh"][b]; tq = ntb["tq"][b]
                rg = res(tmp_pref + "gs%d" % b); rs = res(tmp_pref + "sh%d" % b); rt = res(tmp_pref + "tq%d" % b)
                S.dma("sp", gs[:, :], modb[ty, m_scale][:, q * 1024:(q + 1) * 1024], slot(tmp_pref + "gs%d" % b),
                      reads=[res("modb%d_%d" % (ty, m_scale))], writes=[rg])
                S.dma("sp", sh[:, :], modb[ty, m_shift][:, q * 1024:(q + 1) * 1024], slot(tmp_pref + "sh%d" % b),
                      reads=[res("modb%d_%d" % (ty, m_shift))], writes=[rs])
                S.op("dve", lambda e, q=q, gs=gs, tq=tq: e.scalar_tensor_tensor(tq[:, :], src[:, q * 1024:(q + 1) * 1024], st[:, 7:8], gs[:, :], OP.mult, OP.mult),
                     reads=[src_res, res("st7"), rg], writes=[rt])
                S.op("dve", lambda e, tq=tq, sh=sh: e.tensor_tensor(tq[:, :], tq[:, :], sh[:, :], OP.add), reads=[rt, rs], writes=[rt])
                for h2 in range(2):
                    pb = PS[6 + h2]
                    for j in range(4):
                        S.op("pe", lambda e, pb=pb, j=j, tq=tq, h2=h2: e.transpose(pb[:, j * 128:(j + 1) * 128], tq[:, (h2 * 4 + j) * 128:(h2 * 4 + j + 1) * 128], ident[:, :]),
                             reads=[rt, r_const], writes=[psr[6 + h2]], inc=(j == 3))
                    kc0 = q * 8 + h2 * 4
                    eng = "act" if h2 == 0 else "dve"
                    if eng == "act":
                        S.op("act", lambda e, pb=pb, kc0=kc0: e.activation(dstT[:, kc0:kc0 + 4, col0:col0 + 128], pb[:, :].rearrange("p (a b) -> p a b", a=4), AF.Copy),
                             reads=[psr[6 + h2]], writes=[dst_res])
                    else:
                        S.op("dve", lambda e, pb=pb, kc0=kc0: e.tensor_copy(dstT[:, kc0:kc0 + 4, col0:col0 + 128], pb[:, :].rearrange("p (a b) -> p a b", a=4)),
                             reads=[psr[6 + h2]], writes=[dst_res])

        out_slots = []
        groups = [
            dict(kv=0, tiles=[("p", 128), ("p", 256), ("p", 384), ("p", 512)], own0=1, ty=0),
            dict(kv=512, tiles=[("p", 640), ("p", 768), ("p", 896)], own0=0, ty=0),
            dict(kv=896, tiles=[("p", 1024), ("p", 1152), ("s", 0)], own0=0, ty=1),
        ]
        xmid_idx = 0
        def stage(n):
            if n >= K_STOP:
                raise _StopBuild()
        try:
          stage(1)
          for gi, G in enumerate(groups):
              if str(gi) not in os.environ.get("K_GROUPS", "012"):
                  continue
              tiles = G["tiles"]; NT = len(tiles); NP = NT * 128
              own0 = G["own0"]; NOWN = NT - own0
              has_s = tiles[-1][0] == "s"
              NPP = NP - (128 if has_s else 0)
              fence_all(["hT", "xt0", "xt1", "nt1tq0", "nt1tq1", "nt1gs0", "nt1gs1", "nt1sh0", "nt1sh1", "kT", "Vt"])
              alltiles = [("p", G["kv"])] + tiles
              for ti, (kind, row) in enumerate(alltiles):
                  b = ti % 2
                  srcd = xp[row:row + 128, :] if kind == "p" else xs[:, :]
                  S.dma("sp", xt[b][:, :], srcd, slot("xt%d" % b), writes=[res("xt%d" % b)])
                  norm_tile(xt[b], res("xt%d" % b), 1 if kind == "s" else 0, 1, 0, nt1, hT, res("hT"), ti * 128, "nt1")
              stage(2 + 10 * gi)
              fence_all(["slab2_0", "slab2_1", "slab2_2", "qT", "cvT", "PT0", "PT1", "PT2", "PT3", "rec0", "rec1",
                         "cwk0", "cwk1", "cwk2", "cwk3", "cwk4", "kcT", "vcs"])
              CT = 128 + NP

              def load_slab2(c0, src=w_in_v):
                  i = slab_ctr["s2"] % 3; slab_ctr["s2"] += 1
                  sl = slab2[i].rearrange("p (k c) -> p k c", k=32)
                  S.dma("pool", sl, src[:, :, c0:c0 + 256], slot("slab2_%d" % i), writes=[res("slab2_%d" % i)])
                  return sl, res("slab2_%d" % i)

              slK, rK = load_slab2(2048)
              for g in range(4):
                  for (c0, cn, pbi) in ((0, min(512, CT), 0), (512, CT - 512, 1)):
                      if cn <= 0:
                          continue
                      pb = PS[pbi]
                      for half in range(2):
                          for kc in range(KC):
                              S.op("pe", lambda e, pb=pb, half=half, kc=kc, g=g, c0=c0, cn=cn: e.matmul(
                                  pb[half * 64:(half + 1) * 64, 0:cn], slK[:, kc, g * 64:(g + 1) * 64], hT[:, kc, c0:c0 + cn],
                                  start=(kc == 0), stop=(kc == KC - 1)), reads=[res("hT"), rK], writes=[psr[pbi]], inc=(half == 1 and kc == KC - 1))
                      S.op("act", lambda e, pb=pb, g=g, c0=c0, cn=cn: e.activation(kT[:, g, c0:c0 + cn], pb[:, 0:cn], AF.Copy),
                           reads=[psr[pbi]], writes=[res("kT")])
              slV, rV = load_slab2(2304)
              for ti in range(NT + 1):
                  pb = PS[2 + ti % 2]
                  for kc in range(KC):
                      S.op("pe", lambda e, pb=pb, kc=kc, ti=ti: e.matmul(pb[:, 0:256], hT[:, kc, ti * 128:(ti + 1) * 128], slV[:, kc, :], start=(kc == 0), stop=(kc == KC - 1)),
                           reads=[res("hT"), rV], writes=[psr[2 + ti % 2]], inc=(kc == KC - 1))
                  S.op("act", lambda e, pb=pb, ti=ti: e.activation(Vt[:, ti, :], pb[:, 0:256], AF.Copy), reads=[psr[2 + ti % 2]], writes=[res("Vt")])
                  is_out = (gi == 2 and ti in (2, 3) and "o" not in os.environ.get("K_SKIP", ""))
                  if is_out:
                      oi = ti - 2
                      S.op("dve", lambda e, pb=pb, oi=oi: e.tensor_copy(kvo[:, oi, 256:512], pb[:, 0:256]), reads=[psr[2 + ti % 2]], writes=[res("kvo%d" % oi)])
                      pk = PS[4 + oi]
                      for kc in range(KC):
                          S.op("pe", lambda e, pk=pk, kc=kc, ti=ti: e.matmul(pk[:, 0:256], hT[:, kc, ti * 128:(ti + 1) * 128], slK[:, kc, :], start=(kc == 0), stop=(kc == KC - 1)),
                               reads=[res("hT"), rK], writes=[psr[4 + oi]], inc=(kc == KC - 1))
                      S.op("dve", lambda e, pk=pk, oi=oi: e.tensor_copy(kvo[:, oi, 0:256], pk[:, 0:256]), reads=[psr[4 + oi]], writes=[res("kvo%d" % oi)])
              KS = os.environ.get("K_SKIP", "")
              if gi == 2 and "k" not in KS:
                  so = slot("kvout"); out_slots.append(so)
                  S.dma("sp", kwin_p[:, :], kvo[:, 0, 0:256], so, reads=[res("kvo0")])
                  S.dma("sp", vwin_p[:, :], kvo[:, 0, 256:512], so, reads=[res("kvo0")])
                  for sq in range(16):
                      S.dma("sp", kwin_s[sq, 120:128, :], kvo[sq * 8:(sq + 1) * 8, 1, 0:256], so, reads=[res("kvo1")])
                      S.dma("sp", vwin_s[sq, 120:128, :], kvo[sq * 8:(sq + 1) * 8, 1, 256:512], so, reads=[res("kvo1")])
                  if not os.environ.get("K_NOD2D"):
                      S.dma("sp", kwin_s[:, 0:120, :], kc_d[:, 8:128, :], so)
                      S.dma("sp", vwin_s[:, 0:120, :], vc_d[:, 8:128, :], so)
              for sq in range(8):
                  sl, rs_ = load_slab2(sq * 256)
                  for c in range(2):
                      ch = sq * 2 + c
                      pb = PS[ch % 2]
                      for kc in range(KC):
                          S.op("pe", lambda e, pb=pb, kc=kc, c=c, sl=sl: e.matmul(pb[:, 0:NP], sl[:, kc, c * 128:(c + 1) * 128], hT[:, kc, 128:128 + NP], start=(kc == 0), stop=(kc == KC - 1)),
                               reads=[res("hT"), rs_], writes=[psr[ch % 2]], inc=(kc == KC - 1))
                      S.op("act", lambda e, pb=pb, ch=ch: e.activation(qT[:, ch, 0:NP], pb[:, 0:NP], AF.Copy), reads=[psr[ch % 2]], writes=[res("qT")])
              for cj in range(8):
                  slB, rB = load_slab2(2560 + cj * 256)
                  slC, rC = load_slab2(4608 + cj * 256)
                  slH, rH = load_slab2(6656 + cj * 256)
                  for c in range(2):
                      ch = cj * 2 + c
                      base = 2 + 3 * (ch % 2) if False else (2 if ch % 2 == 0 else 5)
                      bks = (2, 3, 4) if ch % 2 == 0 else (5, 0, 1)
                      for (sl, rr, bk) in ((slB, rB, bks[0]), (slC, rC, bks[1]), (slH, rH, bks[2])):
                          for kc in range(KC):
                              S.op("pe", lambda e, bk=bk, kc=kc, c=c, sl=sl: e.matmul(PS[bk][:, 0:NP], sl[:, kc, c * 128:(c + 1) * 128], hT[:, kc, 128:128 + NP], start=(kc == 0), stop=(kc == KC - 1)),
                                   reads=[res("hT"), rr], writes=[psr[bk]], inc=(kc == KC - 1))
                      pB, pC, pH = PS[bks[0]], PS[bks[1]], PS[bks[2]]
                      hcs, ub, t1, t2 = cwk[0], cwk[1], cwk[2], cwk[3]
                      S.op("act", lambda e, pH=pH: e.activation(hcs[:, 0:NP], pH[:, 0:NP], AF.Copy), reads=[psr[bks[2]]], writes=[res("cwk0")])
                      if NPP > 0:
                          S.op("dve", lambda e, ch=ch: e.tensor_copy(wk[0][:, 0:2], ucarry[:, ch, :]), reads=[res("ucarry")], writes=[wkr[0]])
                          S.op("dve", lambda e, pC=pC: e.tensor_tensor(wk[0][:, 2:2 + NPP], pC[:, 0:NPP], hcs[:, 0:NPP], OP.mult),
                               reads=[psr[bks[1]], res("cwk0")], writes=[wkr[0]])
                          if gi == 0:
                              S.op("dve", lambda e: e.tensor_scalar(wk[0][:, 2:130], wk[0][:, 2:130], flag[:, 0:1], None, OP.mult), reads=[wkr[0], r_const], writes=[wkr[0]])
                          S.op("act", lambda e, ch=ch: e.activation(ucarry[:, ch, :], wk[0][:, NPP:NPP + 2], AF.Copy), reads=[wkr[0]], writes=[res("ucarry")])
                          S.op("dve", lambda e, ch=ch: e.tensor_scalar(t1[:, 0:NPP], wk[0][:, 0:NPP], cw[:, ch, 0:1], None, OP.mult), reads=[wkr[0], r_const], writes=[res("cwk2")])
                          S.op("dve", lambda e, ch=ch: e.scalar_tensor_tensor(t2[:, 0:NPP], wk[0][:, 1:1 + NPP], cw[:, ch, 1:2], t1[:, 0:NPP], OP.mult, OP.add), reads=[wkr[0], res("cwk2")], writes=[res("cwk3")])
                          S.op("dve", lambda e, ch=ch: e.scalar_tensor_tensor(t1[:, 0:NPP], wk[0][:, 2:2 + NPP], cw[:, ch, 2:3], t2[:, 0:NPP], OP.mult, OP.add), reads=[wkr[0], res("cwk3")], writes=[res("cwk2")])
                          S.op("dve", lambda e, ch=ch, pB=pB: e.tensor_tensor(cvT[:, ch, 0:NPP], pB[:, 0:NPP], t1[:, 0:NPP], OP.mult), reads=[psr[bks[0]], res("cwk2")], writes=[res("cvT")])
                      if has_s and "s" not in KS:
                          s0 = NPP
                          u3 = wk[1][:, 0:160].rearrange("p (s t) -> p s t", s=16)
                          S.op("dve", lambda e, ch=ch: e.tensor_copy(u3[:, :, 0:2], ush[:, ch, :, :]), reads=[r_const], writes=[wkr[1]])
                          S.op("dve", lambda e, pC=pC: e.tensor_tensor(u3[:, :, 2:10], pC[:, s0:s0 + 128].rearrange("p (s t) -> p s t", s=16), hcs[:, s0:s0 + 128].rearrange("p (s t) -> p s t", s=16), OP.mult),
                               reads=[psr[bks[1]], res("cwk0")], writes=[wkr[1]])
                          S.op("act", lambda e, ch=ch: e.activation(usl[:, ch, :, :], u3[:, :, 8:10], AF.Copy), reads=[wkr[1]], writes=[res("usl")])
                          t13 = wk[2][:, 0:128].rearrange("p (s t) -> p s t", s=16); t23 = wk[3][:, 0:128].rearrange("p (s t) -> p s t", s=16)
                          S.op("dve", lambda e, ch=ch: e.tensor_scalar(t13, u3[:, :, 0:8], cw[:, ch, 0:1], None, OP.mult), reads=[wkr[1], r_const], writes=[wkr[2]])
                          S.op("dve", lambda e, ch=ch: e.scalar_tensor_tensor(t23, u3[:, :, 1:9], cw[:, ch, 1:2], t13, OP.mult, OP.add), reads=[wkr[1], wkr[2]], writes=[wkr[3]])
                          S.op("dve", lambda e, ch=ch: e.scalar_tensor_tensor(t13, u3[:, :, 2:10], cw[:, ch, 2:3], t23, OP.mult, OP.add), reads=[wkr[1], wkr[3]], writes=[wkr[2]])
                          S.op("dve", lambda e, ch=ch, pB=pB: e.tensor_tensor(cvT[:, ch, s0:s0 + 128], pB[:, s0:s0 + 128], wk[2][:, 0:128], OP.mult), reads=[psr[bks[0]], wkr[2]], writes=[res("cvT")])
              if gi == 2 and "c" not in KS:
                  so = slot("convout"); out_slots.append(so)
                  S.dma("sp", conv_p, ucarry[:], so, reads=[res("ucarry")])
                  S.dma("sp", conv_s, usl[:], so, reads=[res("usl")])
              stage(3 + 10 * gi)
              for tj in range(NT):
                  kind = tiles[tj][0]
                  qc0 = tj * 128
                  kcur = 128 + tj * 128
                  kprev = tj * 128
                  if kind == "p":
                      mcur = 0
                      mprev = 2 if (gi == 0 and tj == 1) else 1
                  else:
                      mcur = 3
                  for g in range(4):
                      blks = ("prev", "cur") if kind == "p" else ("cur",)
                      pts = {}
                      bi = 0
                      for blk in blks:
                          kcol = kprev if blk == "prev" else kcur
                          for par in range(2):
                              pb = PS[bi]; pt = PT[bi]
                              for jh in range(4):
                                  chq = 4 * g + jh
                                  S.op("pe", lambda e, pb=pb, par=par, jh=jh, chq=chq, kcol=kcol, g=g: e.matmul(
                                      pb[:, jh * 128:(jh + 1) * 128], kT[par * 64:(par + 1) * 64, g, kcol:kcol + 128], qT[par * 64:(par + 1) * 64, chq, qc0:qc0 + 128], start=True, stop=True),
                                      reads=[res("kT"), res("qT")], writes=[psr[bi]], inc=(jh == 3))
                              S.op("act", lambda e, pb=pb, pt=pt: e.activation(pt[:, :], pb[:, :], AF.Exp, scale=0.125), reads=[psr[bi]], writes=[res("PT%d" % bi)])
                              mi = (mprev if blk == "prev" else mcur)
                              S.op("dve", lambda e, pt=pt, mi=mi: e.tensor_tensor(pt[:, :], pt[:, :], masks[:, mi, :], OP.mult), reads=[res("PT%d" % bi), res("masks")], writes=[res("PT%d" % bi)])
                              pts[(blk, par)] = bi
                              bi += 1
                      pO, pD = PS[4], PS[5]
                      if kind == "p":
                          for par in range(2):
                              for (pbk, is_o) in ((pO, True), (pD, False)):
                                  for n_, blk in enumerate(blks):
                                      vti = tj if blk == "prev" else tj + 1
                                      bi = pts[(blk, par)]
                                      lhs = (lambda vti=vti, g=g: Vt[:, vti, g * 64:(g + 1) * 64]) if is_o else (lambda: onesb[:, :])
                                      S.op("pe", lambda e, pbk=pbk, par=par, lhs=lhs, bi=bi, n_=n_: e.matmul(pbk[par * 64:(par + 1) * 64, :], lhs(), PT[bi][:, :], start=(n_ == 0), stop=(n_ == 1)),
                                           reads=[res("Vt"), res("PT%d" % bi), res("ones")], writes=[psr[4 if is_o else 5]], inc=(n_ == 1))
                      else:
                          for hs in range(2):
                              S.dma("pool", kcT, kcT_d[:, hs * 8:(hs + 1) * 8, :, :], slot("kcT"), writes=[res("kcT")])
                              S.dma("pool", vcs, vc_d[hs * 8:(hs + 1) * 8, :, :].rearrange("s p c -> p s c"), slot("vcs"), writes=[res("vcs")])
                              for par in range(2):
                                  pb = PS[2 + par]; pt = PT[2 + par]
                                  for s8 in range(8):
                                      sq = hs * 8 + s8
                                      for jh in range(4):
                                          S.op("pe", lambda e: e.matmul(
                                              pb[:, s8 * 32 + jh * 8:s8 * 32 + jh * 8 + 8], kcT[par * 64:(par + 1) * 64, s8, g, :], qT[par * 64:(par + 1) * 64, 4 * g + jh, qc0 + sq * 8:qc0 + sq * 8 + 8], start=True, stop=True),
                                              reads=[res("kcT"), res("qT")], writes=[psr[2 + par]], inc=(s8 == 7 and jh == 3))
                                  S.op("act", lambda e, pb=pb, pt=pt: e.activation(pt[:, 0:256], pb[:, 0:256], AF.Exp, scale=0.125), reads=[psr[2 + par]], writes=[res("PT%d" % (2 + par))])
                                  S.op("dve", lambda e, pt=pt: e.tensor_tensor(pt[:, 0:256], pt[:, 0:256], masks[:, 4, 0:256], OP.mult), reads=[res("PT%d" % (2 + par)), res("masks")], writes=[res("PT%d" % (2 + par))])
                                  for s8 in range(8):
                                      sq = hs * 8 + s8
                                      last = (hs == 1 and s8 == 7)
                                      for jh in range(4):
                                          oc = jh * 128 + sq * 8
                                          pc = s8 * 32 + jh * 8
                                          fst = (hs == 0 and s8 == 0 and jh == 0)
                                          S.op("pe", lambda e: e.matmul(pO[par * 64:(par + 1) * 64, oc:oc + 8], vcs[:, s8, g * 64:(g + 1) * 64], pt[:, pc:pc + 8], start=fst, stop=False, skip_group_check=True),
                                               reads=[res("vcs"), res("PT%d" % (2 + par))], writes=[psr[4]], inc=(s8 == 7 and jh == 3))
                                          S.op("pe", lambda e: e.matmul(pD[par * 64:(par + 1) * 64, oc:oc + 8], onesb[:, :], pt[:, pc:pc + 8], start=fst, stop=False, skip_group_check=True),
                                               reads=[res("ones"), res("PT%d" % (2 + par))], writes=[psr[5]], inc=(s8 == 7 and jh == 3))
                      if kind != "p":
                          for par in range(2):
                              bi = pts[("cur", par)]
                              S.op("pe", lambda e: e.matmul(pO[par * 64:(par + 1) * 64, :], Vt[:, tj + 1, g * 64:(g + 1) * 64], PT[bi][:, :], start=False, stop=True, skip_group_check=True),
                                   reads=[res("Vt"), res("PT%d" % bi)], writes=[psr[4]])
                              S.op("pe", lambda e: e.matmul(pD[par * 64:(par + 1) * 64, :], onesb[:, :], PT[bi][:, :], start=False, stop=True, skip_group_check=True),
                                   reads=[res("ones"), res("PT%d" % bi)], writes=[psr[5]])
                      S.op("dve", lambda e, g=g: e.tensor_tensor(rec[0][:, :].rearrange("p (j q) -> p j q", j=4), pD[:, :].rearrange("p (j q) -> p j q", j=4),
                                                               esk[:, 4 * g:4 * g + 4].unsqueeze(2).to_broadcast([128, 4, 128]), OP.add),
                           reads=[psr[5], res("esk")], writes=[res("rec0")])
                      S.op("dve", lambda e: e.reciprocal(rec[1][:, :], rec[0][:, :]), reads=[res("rec0")], writes=[res("rec1")])
                      S.op("dve", lambda e, g=g, qc0=qc0: e.tensor_tensor(qT[:, 4 * g:4 * g + 4, qc0:qc0 + 128], pO[:, :].rearrange("p (j q) -> p j q", j=4), rec[1][:, :].rearrange("p (j q) -> p j q", j=4), OP.mult),
                           reads=[psr[4], res("rec1")], writes=[res("qT")])
              stage(4 + 10 * gi)
              fence_all(["xm0", "xm1", "xm2", "xm3", "slab3_0", "slab3_1", "gpc0", "gpc1",
                         "nt3tq0", "nt3tq1", "nt3gs0", "nt3gs1", "nt3sh0", "nt3sh1"])
              for tj, (kind, row) in enumerate(tiles):
                  srcd = xp[row:row + 128, :] if kind == "p" else xs[:, :]
                  S.dma("sp", xm[:, tj, :], srcd, slot("xm%d" % tj), writes=[res("xm%d" % tj)])
              for cb in range(8):
                  sls = []
                  for half in range(2):
                      i = slab_ctr["s3"] % 2; slab_ctr["s3"] += 1
                      sl = slab3[i].rearrange("p (k c) -> p k c", k=16)
                      S.dma("pool", sl, w_out_v[:, half * 16:(half + 1) * 16, cb * 512:(cb + 1) * 512], slot("slab3_%d" % i), writes=[res("slab3_%d" % i)])
                      for tj in range(NT):
                          for k in range(16):
                              kc = half * 16 + k
                              src = qT if kc < 16 else cvT
                              S.op("pe", lambda e, tj=tj, kc=kc, k=k, sl=sl, src=src: e.matmul(PS[tj][:, :], src[:, kc % 16, tj * 128:(tj + 1) * 128], sl[:, k, :], start=(kc == 0), stop=(kc == KC - 1)),
                                   reads=[res("qT"), res("cvT"), res("slab3_%d" % i)], writes=[psr[tj]], inc=(k == 15))
                  for ty in sorted(set(1 if t[0] == "s" else 0 for t in tiles)):
                      gi_ = ty
                      S.dma("sp", gpc[gi_][:, :], modb[ty, 2][:, cb * 512:(cb + 1) * 512], slot("gpc%d" % gi_), reads=[res("modb%d_2" % ty)], writes=[res("gpc%d" % gi_)])
                  for tj, (kind, row) in enumerate(tiles):
                      ty = 1 if kind == "s" else 0
                      S.op("dve", lambda e, tj=tj, ty=ty: e.tensor_tensor(wk[4][:, 0:512], PS[tj][:, :], gpc[ty][:, :], OP.mult), reads=[psr[tj], res("gpc%d" % ty)], writes=[wkr[4]])
                      S.op("dve", lambda e, tj=tj, cb=cb: e.tensor_tensor(xm[:, tj, cb * 512:(cb + 1) * 512], xm[:, tj, cb * 512:(cb + 1) * 512], wk[4][:, 0:512], OP.add),
                           reads=[wkr[4], res("xm%d" % tj)], writes=[res("xm%d" % tj)])
              fence(["qT", "cvT"], ["zT"])
              xmid_of = {}
              for tj, (kind, row) in enumerate(tiles):
                  ty = 1 if kind == "s" else 0
                  norm_tile(xm[:, tj, :], res("xm%d" % tj), ty, 4, 3, nt3, zT, res("zT"), tj * 128, "nt3")
                  if tj >= own0:
                      S.dma("sp", xmid[xmid_idx], xm[:, tj, :], slot("xmsp%d" % tj), reads=[res("xm%d" % tj)], writes=[res("xmid%d" % xmid_idx)])
                      xmid_of[tj] = xmid_idx
                      xmid_idx += 1
              stage(5 + 10 * gi)
              fence_all(["aT", "slab2_0", "slab2_1", "slab2_2", "fwk"])
              OC0 = own0 * 128; NO = NOWN * 128
              for fj in range(FC // 2):
                  slG, rG = load_slab2(fj * 256, w_gu_v)
                  slU, rU = load_slab2(DFF + fj * 256, w_gu_v)
                  for c in range(2):
                      fc = fj * 2 + c
                      pG = PS[fc % 2]; pU = PS[2 + fc % 2]
                      for kc in range(KC):
                          S.op("pe", lambda e, pG=pG, kc=kc, c=c: e.matmul(pG[:, 0:NP], slG[:, kc, c * 128:(c + 1) * 128], zT[:, kc, 0:NP], start=(kc == 0), stop=(kc == KC - 1)),
                               reads=[res("zT"), rG], writes=[psr[fc % 2]], inc=(kc == KC - 1))
                      for kc in range(KC):
                          S.op("pe", lambda e, pU=pU, kc=kc, c=c: e.matmul(pU[:, 0:NO], slU[:, kc, c * 128:(c + 1) * 128], zT[:, kc, OC0:OC0 + NO], start=(kc == 0), stop=(kc == KC - 1)),
                               reads=[res("zT"), rU], writes=[psr[2 + fc % 2]], inc=(kc == KC - 1))
                      gp, t1, t2 = wk[0], wk[2], wk[3]
                      if NPP > 0:
                          S.op("act", lambda e, fc=fc: e.activation(gp[:, 0:2], gcarry[:, fc, :], AF.Copy), reads=[res("gcarry")], writes=[wkr[0]])
                          S.op("act", lambda e, pG=pG: e.activation(gp[:, 2:2 + NPP], pG[:, 0:NPP], AF.Copy), reads=[psr[fc % 2]], writes=[wkr[0]])
                          if gi == 0:
                              S.op("dve", lambda e: e.tensor_scalar(gp[:, 2:130], gp[:, 2:130], flag[:, 0:1], None, OP.mult), reads=[wkr[0], r_const], writes=[wkr[0]])
                          S.op("act", lambda e, fc=fc: e.activation(gcarry[:, fc, :], gp[:, NPP:NPP + 2], AF.Copy), reads=[wkr[0]], writes=[res("gcarry")])
                          S.op("dve", lambda e, fc=fc: e.tensor_scalar(t1[:, 0:NPP], gp[:, 0:NPP], fcw[:, fc, 0:1], None, OP.mult), reads=[wkr[0], r_const], writes=[wkr[2]])
                          S.op("dve", lambda e, fc=fc: e.scalar_tensor_tensor(t2[:, 0:NPP], gp[:, 1:1 + NPP], fcw[:, fc, 1:2], t1[:, 0:NPP], OP.mult, OP.add), reads=[wkr[0], wkr[2]], writes=[wkr[3]])
                          S.op("dve", lambda e, fc=fc: e.scalar_tensor_tensor(t1[:, 0:NPP], gp[:, 2:2 + NPP], fcw[:, fc, 2:3], t2[:, 0:NPP], OP.mult, OP.add), reads=[wkr[0], wkr[3]], writes=[wkr[2]])
                          S.op("act", lambda e: e.activation(t2[:, 0:NPP], t1[:, 0:NPP], AF.Silu), reads=[wkr[2]], writes=[wkr[3]])
                          NPO = NPP - OC0
                          S.op("dve", lambda e, fc=fc, pU=pU, NPO=NPO: e.tensor_tensor(aT[:, fc, 0:NPO], pU[:, 0:NPO], t2[:, OC0:OC0 + NPO], OP.mult), reads=[psr[2 + fc % 2], wkr[3]], writes=[res("aT")])
                      if has_s:
                          s0 = NPP
                          g3 = wk[1][:, 0:160].rearrange("p (s t) -> p s t", s=16)
                          S.dma("sp", gq[:, fc % 2, :, :], gph_d[:, fc, :, :], slot("gq%d" % (fc % 2)), writes=[res("gq%d" % (fc % 2))])
                          S.op("act", lambda e, fc=fc: e.activation(g3[:, :, 0:2], gq[:, fc % 2, :, :], AF.Copy), reads=[res("gq%d" % (fc % 2))], writes=[wkr[1]])
                          S.op("act", lambda e, pG=pG: e.activation(g3[:, :, 2:10], pG[:, s0:s0 + 128].rearrange("p (s t) -> p s t", s=16), AF.Copy), reads=[psr[fc % 2]], writes=[wkr[1]])
                          S.op("act", lambda e, fc=fc: e.activation(gso[:, fc % 2, :, :], g3[:, :, 8:10], AF.Copy), reads=[wkr[1]], writes=[res("gso%d" % (fc % 2))])
                          S.dma("sp", ffn_s[:, fc, :, :], gso[:, fc % 2, :, :], slot("gso%d" % (fc % 2)), reads=[res("gso%d" % (fc % 2))])
                          t13 = wk[4][:, 0:128].rearrange("p (s t) -> p s t", s=16); t23 = wk[5][:, 0:128].rearrange("p (s t) -> p s t", s=16)
                          S.op("dve", lambda e, fc=fc: e.tensor_scalar(t13, g3[:, :, 0:8], fcw[:, fc, 0:1], None, OP.mult), reads=[wkr[1], r_const], writes=[wkr[4]])
                          S.op("dve", lambda e, fc=fc: e.scalar_tensor_tensor(t23, g3[:, :, 1:9], fcw[:, fc, 1:2], t13, OP.mult, OP.add), reads=[wkr[1], wkr[4]], writes=[wkr[5]])
                          S.op("dve", lambda e, fc=fc: e.scalar_tensor_tensor(t13, g3[:, :, 2:10], fcw[:, fc, 2:3], t23, OP.mult, OP.add), reads=[wkr[1], wkr[5]], writes=[wkr[4]])
                          S.op("act", lambda e: e.activation(wk[5][:, 0:128], wk[4][:, 0:128], AF.Silu), reads=[wkr[4]], writes=[wkr[5]])
                          S.op("dve", lambda e, fc=fc, pU=pU: e.tensor_tensor(aT[:, fc, s0 - OC0:s0 - OC0 + 128], pU[:, s0 - OC0:s0 - OC0 + 128], wk[5][:, 0:128], OP.mult), reads=[psr[2 + fc % 2], wkr[5]], writes=[res("aT")])
              if gi == 2:
                  so = slot("ffnout"); out_slots.append(so)
                  S.dma("sp", ffn_p, gcarry[:], so, reads=[res("gcarry")])
                  out_slots.append(slot("gso0")); out_slots.append(slot("gso1"))
              stage(6 + 10 * gi)
              fence_all(["xo0", "xo1", "xo2", "slab5_0", "slab5_1", "ytmp", "pc5_0", "pc5_1", "pc5_2", "pc5_3"])
              for oj in range(NOWN):
                  tj = own0 + oj
                  S.dma("sp", xo[:, oj, :], xmid[xmid_of[tj]], slot("xo%d" % oj), reads=[res("xmid%d" % xmid_of[tj])], writes=[res("xo%d" % oj)])
              pieces = [(p0, min(8, FC - p0)) for p0 in range(0, FC, 8)]
              for cb in range(8):
                  bank0 = 3 * (cb % 2)
                  for (p0, pn) in pieces:
                      i = slab_ctr["s5"] % 2; slab_ctr["s5"] += 1
                      sl = slab5[i].rearrange("p (k c) -> p k c", k=8)
                      S.dma("pool", sl[:, 0:pn, :], w_dn_v[:, p0:p0 + pn, cb * 512:(cb + 1) * 512], slot("slab5_%d" % i), writes=[res("slab5_%d" % i)])
                      for oj in range(NOWN):
                          for k in range(pn):
                              fc = p0 + k
                              S.op("pe", lambda e, oj=oj, fc=fc, k=k, sl=sl, bank0=bank0: e.matmul(PS[bank0 + oj][:, :], aT[:, fc, oj * 128:(oj + 1) * 128], sl[:, k, :], start=(fc == 0), stop=(fc == FC - 1)),
                                   reads=[res("aT"), res("slab5_%d" % i)], writes=[psr[bank0 + oj]], inc=(k == pn - 1))
                  for ty in sorted(set(1 if t[0] == "s" else 0 for t in tiles[own0:])):
                      S.dma("sp", pc5[ty][:, :], modb[ty, 5][:, cb * 512:(cb + 1) * 512], slot("pc5_%d" % ty), reads=[res("modb%d_5" % ty)], writes=[res("pc5_%d" % ty)])
                  for oj in range(NOWN):
                      ty = 1 if tiles[own0 + oj][0] == "s" else 0
                      S.op("dve", lambda e, oj=oj, ty=ty, bank0=bank0: e.tensor_tensor(wk[4][:, 0:512], PS[bank0 + oj][:, :], pc5[ty][:, :], OP.mult), reads=[psr[bank0 + oj], res("pc5_%d" % ty)], writes=[wkr[4]])
                      S.op("dve", lambda e, oj=oj, cb=cb: e.tensor_tensor(xo[:, oj, cb * 512:(cb + 1) * 512], xo[:, oj, cb * 512:(cb + 1) * 512], wk[4][:, 0:512], OP.add),
                           reads=[wkr[4], res("xo%d" % oj)], writes=[res("xo%d" % oj)])
              for oj in range(NOWN):
                  kind, row = tiles[own0 + oj]
                  src = xo[:, oj, :]; rsrc = res("xo%d" % oj)
                  st = stats
                  for q in range(4):
                      S.op("act", lambda e, q=q, src=src: e.activation(ytmp[:, 0:1024], src[:, q * 1024:(q + 1) * 1024], AF.Square, accum_out=st[:, q:q + 1]),
                           reads=[rsrc], writes=[res("ytmp"), res("st%d" % q)])
                  S.op("dve", lambda e: e.tensor_reduce(st[:, 4:5], st[:, 0:4], mybir.AxisListType.X, OP.add), reads=[res("st0"), res("st1"), res("st2"), res("st3")], writes=[res("st4")])
                  S.op("dve", lambda e: e.tensor_scalar(st[:, 5:6], st[:, 4:5], 1.0 / D, EPS, OP.mult, OP.add), reads=[res("st4")], writes=[res("st5")])
                  S.op("act", lambda e: e.activation(st[:, 6:7], st[:, 5:6], AF.Sqrt), reads=[res("st5")], writes=[res("st6")])
                  S.op("dve", lambda e: e.reciprocal(st[:, 7:8], st[:, 6:7]), reads=[res("st6")], writes=[res("st7")])
                  for q in range(8):
                      b = 2 + q % 2
                      S.dma("sp", pc5[b][:, :], g_fin[q * 512:(q + 1) * 512].partition_broadcast(128), slot("pc5_%d" % b), writes=[res("pc5_%d" % b)])
                      S.op("dve", lambda e, q=q, b=b, src=src: e.scalar_tensor_tensor(ytmp[:, q * 512:(q + 1) * 512], src[:, q * 512:(q + 1) * 512], st[:, 7:8], pc5[b][:, :], OP.mult, OP.mult),
                           reads=[rsrc, res("st7"), res("pc5_%d" % b)], writes=[res("ytmp")])
                  so = slot("yout");
                  if so not in out_slots:
                      out_slots.append(so)
                  dst = y_s[:, :] if kind == "s" else y_p[row - 256:row - 256 + 128, :]
                  S.dma("sp", dst, ytmp[:, :], so, reads=[res("ytmp")])
        except _StopBuild:
            out_slots = list(slots.values())
        S.wait_all("sp", out_slots)

        with nc.Block() as block:
            @block.tensor
            def _(e):
                for c in S.streams["pe"]:
                    c(e)

            @block.scalar
            def _(e):
                for c in S.streams["act"]:
                    c(e)

            @block.vector
            def _(e):
                for c in S.streams["dve"]:
                    c(e)

            @block.gpsimd
            def _(e):
                for c in S.streams["pool"]:
                    c(e)

            @block.sync
            def _(e):
                for c in S.streams["sp"]:
                    c(e)
    return nc


_CACHE = {}


def _masks():
    m = np.zeros((5, 128, 512), np.float32)
    s = np.arange(128)[:, None]
    q = np.arange(128)[None, :]
    cur = (q >= s).astype(np.float32)
    prev = (s >= q).astype(np.float32)
    seq_eq = ((s // 8) == (q // 8)).astype(np.float32)
    curs = cur * seq_eq
    for j in range(4):
        m[0, :, j * 128:(j + 1) * 128] = cur
        m[1, :, j * 128:(j + 1) * 128] = prev
        m[3, :, j * 128:(j + 1) * 128] = curs
    t = (np.arange(512) % 8)[None, :]
    m[4] = (np.arange(128)[:, None] >= t).astype(np.float32)
    return m


def prep_inputs(x_prompt, x_sample, cache_k_win, cache_v_win, state_conv, state_ffn_conv, c_prompt, c_sample,
                w_mod, b_mod, g_mix, g_ffn, w_in, conv_w, sinks, w_out, w_gate_up, ffn_conv_w, w_down, g_final):
    f = lambda a: np.ascontiguousarray(np.asarray(a, dtype=np.float32))
    x_prompt = f(x_prompt); x_sample = f(x_sample)
    ck = f(cache_k_win)[0]; cv = f(cache_v_win)[0]; sc = f(state_conv)[0]; sf = f(state_ffn_conv)[0]
    c_prompt = f(c_prompt); c_sample = f(c_sample)
    shared = {
        "w_mod": f(w_mod)[0], "b_mod": f(b_mod)[0].reshape(6, D), "g_mix": f(g_mix).reshape(D), "g_ffn": f(g_ffn).reshape(D),
        "g_fin": f(g_final).reshape(D), "w_in": f(w_in)[0], "w_out": f(w_out)[0], "w_gu": f(w_gate_up)[0], "w_dn": f(w_down)[0],
        "cw": f(f(conv_w)[0].reshape(3, 16, 128).transpose(2, 1, 0)),
        "fcw": f(f(ffn_conv_w)[0].reshape(3, FC, 128).transpose(2, 1, 0)),
        "ident": np.eye(128, dtype=np.float32),
    }
    sk = f(sinks)[0]
    snk = np.zeros((128, 16), np.float32)
    for g in range(4):
        for jh in range(4):
            snk[0:64, 4 * g + jh] = sk[8 * g + 2 * jh]
            snk[64:128, 4 * g + jh] = sk[8 * g + 2 * jh + 1]
    shared["snk"] = snk
    sel = np.zeros((18, 2, 128), np.float32)
    sel[0, 0, :] = 1.0; sel[17, :, :] = 1.0
    for t in range(128):
        sel[1 + t // 8, 1, t] = 1.0
    shared["sel"] = sel
    mk = _masks()
    in_maps = []
    for i in range(8):
        n = i // 4; s = (i % 4) * 1024
        xpc = np.zeros((1280, D), np.float32)
        lo = s - 256
        if lo >= 0:
            xpc[:] = x_prompt[n, lo:lo + 1280]
        else:
            xpc[256:] = x_prompt[n, 0:1024]
        sq = slice(16 * i, 16 * i + 16)
        m = mk.copy()
        fl = 1.0 if s > 0 else 0.0
        m[2] = m[1] * fl
        d = dict(shared)
        d.update({
            "xp": xpc, "xs": f(x_sample[sq].reshape(128, D)),
            "c17": f(np.concatenate([c_prompt[n:n + 1], c_sample[sq]], axis=0)),
            "kcT": f(np.concatenate([ck[sq].transpose(3, 0, 2, 1)] * 2, axis=0)),
            "vc": f(cv[sq].reshape(16, 128, 256)), "kc": f(ck[sq].reshape(16, 128, 256)),
            "ush": f(sc[sq].reshape(16, 2, 16, 128).transpose(3, 2, 0, 1)),
            "gph": f(sf[sq].reshape(16, 2, FC, 128).transpose(3, 2, 0, 1)),
            "masks": f(m.transpose(1, 0, 2)), "flag": np.full((128, 1), fl, np.float32),
        })
        in_maps.append(d)
    return in_maps


def kernel(**inputs):
    in_maps = prep_inputs(**inputs)
    if "nc" not in _CACHE:
        _CACHE["nc"] = build_program()
    nc = _CACHE["nc"]
    res = run_bass_kernel_spmd(nc, in_maps, core_ids=list(range(8)))
    return assemble(res.results)


def assemble(R):
    y_prompt = np.zeros((2, 4096, D), np.float32)
    for i in range(8):
        y_prompt[i // 4, (i % 4) * 1024:(i % 4 + 1) * 1024] = R[i]["y_p"]
    y_sample = np.concatenate([R[i]["y_s"].reshape(16, 8, D) for i in range(8)], axis=0)
    kp = np.stack([R[3]["kwin_p"], R[7]["kwin_p"]]).reshape(1, 2, 128, 4, 64)
    vp = np.stack([R[3]["vwin_p"], R[7]["vwin_p"]]).reshape(1, 2, 128, 4, 64)
    cvp = np.stack([R[c]["conv_p"].transpose(2, 1, 0).reshape(2, 2048) for c in (3, 7)])[None]
    ffp = np.stack([R[c]["ffn_p"].transpose(2, 1, 0).reshape(2, DFF) for c in (3, 7)])[None]
    ks = np.concatenate([R[i]["kwin_s"] for i in range(8)], axis=0).reshape(1, 128, 128, 4, 64)
    vs = np.concatenate([R[i]["vwin_s"] for i in range(8)], axis=0).reshape(1, 128, 128, 4, 64)
    cvs = np.concatenate([R[i]["conv_s"].transpose(2, 3, 1, 0).reshape(16, 2, 2048) for i in range(8)], axis=0)[None]
    ffs = np.concatenate([R[i]["ffn_s"].transpose(2, 3, 1, 0).reshape(16, 2, DFF) for i in range(8)], axis=0)[None]
    return (y_prompt, y_sample, np.ascontiguousarray(kp), np.ascontiguousarray(vp), np.ascontiguousarray(cvp), np.ascontiguousarray(ffp),
            np.ascontiguousarray(ks), np.ascontiguousarray(vs), np.ascontiguousarray(cvs), np.ascontiguousarray(ffs))
```

```python
import numpy as np
from contextlib import ExitStack
import concourse.bass as bass
import concourse.mybir as mybir
from concourse.bass_utils import run_bass_kernel_spmd

F32 = mybir.dt.float32
BF = mybir.dt.bfloat16
AF = mybir.ActivationFunctionType
OP = mybir.AluOpType

D = 4096
KC = 32
DFF = 11008
FC = 86
NIN = 8704
EPS = 1e-5
ENGS = ("pe", "act", "dve", "pool", "sp")
import os
K_STOP = int(os.environ.get("K_STOP", "99"))


class _StopBuild(Exception):
    pass


class Res:
    def __init__(self, name=""):
        self.name = name
        self.w = None
        self.r = {}


class Slot:
    def __init__(self, sem):
        self.sem = sem
        self.count = 0


class _Rec:
    def __init__(self):
        self.call = None

    def __getattr__(self, name):
        def f(*a, **k):
            self.call = (name, a, k)
            return self
        return f


class Sched:
    def __init__(self):
        self.streams = {k: [] for k in ENGS}
        self.count = {k: 0 for k in ENGS}
        self.sems = {}
        self.waited = {k: {} for k in ENGS}

    def _need(self, eng, ev):
        if ev is None:
            return
        sem, val, src = ev
        if src == eng and eng == "pe":
            return
        key = id(sem)
        if self.waited[eng].get(key, 0) >= val:
            return
        self.waited[eng][key] = val
        self.streams[eng].append(lambda e, sem=sem, val=val: e.wait_ge(sem, val))

    def _deps(self, eng, reads, writes):
        for r in reads:
            self._need(eng, r.w)
        for w in writes:
            self._need(eng, w.w)
            for ev in list(w.r.values()):
                self._need(eng, ev)

    def _commit(self, ev, reads, writes):
        for r in reads:
            r.r[id(ev[0])] = ev
        for w in writes:
            w.w = ev
            w.r = {}

    def op(self, eng, fn, reads=(), writes=(), inc=True):
        writes = list(writes) + [r for r in reads if r.name.startswith("ps")]
        reads = [r for r in reads if not r.name.startswith("ps")]
        self._deps(eng, reads, writes)
        if not inc:
            rec = _Rec()
            fn(rec)
            name, a, k = rec.call
            self.streams[eng].append(lambda e, name=name, a=a, k=k: getattr(e, name)(*a, **k))
            return
        self.count[eng] += 1
        sem = self.sems[eng]
        ev = (sem, self.count[eng], eng)
        rec = _Rec()
        fn(rec)
        name, a, k = rec.call
        self.streams[eng].append(lambda e, name=name, a=a, k=k, sem=sem: getattr(e, name)(*a, **k).then_inc(sem, 1))
        self._commit(ev, reads, writes)

    def dma(self, q, out, in_, slot, reads=(), writes=(), **kw):
        self._deps(q, reads, writes)
        slot.count += 16
        ev = (slot.sem, slot.count, "dma")
        self.streams[q].append(
            lambda e, out=out, in_=in_, s=slot.sem, kw=kw: e.dma_start(out=out, in_=in_, **kw).then_inc(s, 16))
        self._commit(ev, reads, writes)

    def wait_all(self, eng, slots):
        for s in slots:
            if s.count:
                self._need(eng, (s.sem, s.count, "dma"))


def build_program():
    nc = bass.Bass("TRN2", target_bir_lowering=False)

    def din(name, shape):
        return nc.dram_tensor(name, list(shape), F32, kind="ExternalInput").ap()

    def dout(name, shape):
        return nc.dram_tensor(name, list(shape), F32, kind="ExternalOutput").ap()

    xp = din("xp", [1280, D]); xs = din("xs", [128, D]); c17 = din("c17", [17, D])
    kcT_d = din("kcT", [128, 16, 4, 128]); vc_d = din("vc", [16, 128, 256]); kc_d = din("kc", [16, 128, 256])
    ush_d = din("ush", [128, 16, 16, 2]); gph_d = din("gph", [128, FC, 16, 2])
    w_mod = din("w_mod", [D, 6 * D]); b_mod = din("b_mod", [6, D])
    g_mix = din("g_mix", [D]); g_ffn = din("g_ffn", [D]); g_fin = din("g_fin", [D])
    w_in = din("w_in", [D, NIN]); w_out = din("w_out", [D, D]); w_gu = din("w_gu", [D, 2 * DFF]); w_dn = din("w_dn", [DFF, D])
    cw_d = din("cw", [128, 16, 3]); fcw_d = din("fcw", [128, FC, 3]); snk_d = din("snk", [128, 16])
    ident_d = din("ident", [128, 128]); masks_d = din("masks", [128, 5, 512]); sel_d = din("sel", [18, 2, 128])
    flag_d = din("flag", [128, 1])

    y_p = dout("y_p", [1024, D]); y_s = dout("y_s", [128, D])
    kwin_p = dout("kwin_p", [128, 256]); vwin_p = dout("vwin_p", [128, 256])
    conv_p = dout("conv_p", [128, 16, 2]); ffn_p = dout("ffn_p", [128, FC, 2])
    kwin_s = dout("kwin_s", [16, 128, 256]); vwin_s = dout("vwin_s", [16, 128, 256])
    conv_s = dout("conv_s", [128, 16, 16, 2]); ffn_s = dout("ffn_s", [128, FC, 16, 2])

    modb = nc.dram_tensor("modb", [2, 6, 128, D], F32).ap()
    xmid = nc.dram_tensor("xmid", [10, 128, D], F32).ap()

    S = Sched()
    es = ExitStack()
    with es:
        POOLW = 162 * 256
        pool = es.enter_context(nc.sbuf_tensor("pool", [128, POOLW], F32))

        def fv(kb0, nwords, shape=None):
            o = int(kb0 * 256)
            ap = pool[:, o:o + nwords]
            return ap

        def bv(kb0, nelem):
            o = int(kb0 * 256)
            return pool[:, o:o + nelem // 2].bitcast(BF)

        def sbt(name, shape, dtype=F32):
            return es.enter_context(nc.sbuf_tensor("sb_" + name, list(shape), dtype))

        ident = sbt("ident", [128, 128]); masks = sbt("masks", [128, 5, 512], BF)
        sel = sbt("sel", [18, 2, 128]); flag = sbt("flag", [128, 1])
        cw = sbt("cw", [128, 16, 3]); fcw = sbt("fcw", [128, FC, 3]); esk = sbt("esk", [128, 16])
        ucarry = sbt("ucarry", [128, 16, 2]); gcarry = sbt("gcarry", [128, FC, 2])
        ush = sbt("ush", [128, 16, 16, 2]); gq = sbt("gq", [128, 2, 16, 2])
        usl = sbt("usl", [128, 16, 16, 2]); gso = sbt("gso", [128, 2, 16, 2])
        kvo = sbt("kvo", [128, 2, 512]); stats = sbt("stats", [128, 8])
        cT = sbt("cT", [128, KC, 17], BF); onesb = sbt("onesb", [128, 64], BF)
        wk = [sbt("wk%d" % i, [128, 516]) for i in range(6)]
        PS = [es.enter_context(nc.psum_tensor("ps%d" % i, [128, 512], F32)) for i in range(8)]
        for k in ENGS:
            S.sems[k] = es.enter_context(nc.semaphore("sem_" + k))
        slots = {}

        def slot(name):
            if name not in slots:
                slots[name] = Slot(es.enter_context(nc.semaphore("d_" + name)))
            return slots[name]

        R = {}

        PERSIST = ("const", "masks", "ones", "ucarry", "gcarry", "esk", "usl", "gso", "gq", "kvo", "st", "cT", "wk", "ps", "modb", "xmid")
        FEN = {"evs": {}}

        def is_pool(name):
            return not any(name.startswith(p) for p in PERSIST)

        def res(name):
            if name not in R:
                R[name] = Res(name)
                if is_pool(name):
                    for ev in FEN["evs"].values():
                        R[name].r[id(ev[0])] = ev
            return R[name]

        def fence_all(*_a):
            evs = dict(FEN["evs"])
            for n, rr in R.items():
                if not is_pool(n):
                    continue
                cand = list(rr.r.values())
                if rr.w is not None:
                    cand.append(rr.w)
                for ev in cand:
                    k = id(ev[0])
                    if k not in evs or evs[k][1] < ev[1]:
                        evs[k] = ev
            FEN["evs"] = evs
            for n, rr in R.items():
                if not is_pool(n):
                    continue
                for k, ev in evs.items():
                    if k not in rr.r or rr.r[k][1] < ev[1]:
                        rr.r[k] = ev

        def fence(a, b):
            fence_all()

        psr = [res("ps%d" % i) for i in range(8)]
        wkr = [res("wk%d" % i) for i in range(6)]

        def W32(kb):
            return int(kb * 256)

        su = slot("setup")
        r_const = res("const")
        for (t, d) in ((ident, ident_d), (sel, sel_d), (flag, flag_d), (cw, cw_d), (fcw, fcw_d),
                       (ush, ush_d)):
            S.dma("sp", t[:], d, su, writes=[r_const])
        S.dma("sp", esk[:], snk_d, su, writes=[r_const])
        S.dma("pool", masks[:], masks_d, slot("setup2"), writes=[res("masks")])
        S.op("dve", lambda e: e.memset(onesb[:], 1.0), writes=[res("ones")])
        S.op("dve", lambda e: e.memset(ucarry[:], 0.0), writes=[res("ucarry")])
        S.op("dve", lambda e: e.memset(gcarry[:], 0.0), writes=[res("gcarry")])
        S.op("act", lambda e: e.activation(esk[:], esk[:], AF.Exp), reads=[r_const], writes=[res("esk")])

        csl = fv(98, D)
        mrow = fv(114, D)
        gb = fv(130, D)
        mst = [fv(146, D), fv(0, D)]
        slabA = [bv(16, 16 * 512), bv(32, 16 * 512), bv(48, 16 * 512)]
        S.dma("sp", csl[0:17, :], c17, slot("c17"), writes=[res("csl")])
        S.op("act", lambda e: e.activation(csl[0:17, :], csl[0:17, :], AF.Silu), reads=[res("csl")], writes=[res("csl")])
        for q4 in range(8):
            pb = PS[q4 % 2]
            for j in range(4):
                kc = q4 * 4 + j
                S.op("pe", lambda e, kc=kc, j=j, pb=pb: e.transpose(pb[:, j * 17:(j + 1) * 17], csl[0:17, kc * 128:(kc + 1) * 128], ident[0:17, 0:17]),
                     reads=[res("csl"), r_const], writes=[psr[q4 % 2]], inc=(j == 3))
            S.op("dve", lambda e, q4=q4, pb=pb: e.tensor_copy(cT[:, q4 * 4:(q4 + 1) * 4, :], pb[:, 0:68].rearrange("p (a b) -> p a b", a=4)),
                 reads=[psr[q4 % 2]], writes=[res("cT")])
        wmod_v = w_mod.rearrange("(kc p) c -> p kc c", p=128)
        sidx = 0
        for m in range(6):
            S.dma("sp", mrow[17:18, :], b_mod[m:m + 1, :], slot("bmod"), writes=[res("mrow")])
            if m in (1, 4):
                S.dma("sp", gb[:, :], (g_mix if m == 1 else g_ffn)[:].partition_broadcast(128), slot("gb"), writes=[res("gb")])
            for cb in range(8):
                pb = PS[2 + cb % 2]
                for half in range(2):
                    sl = slabA[sidx % 3]; sr = res("slabA%d" % (sidx % 3)); ss = slot("slabA%d" % (sidx % 3)); sidx += 1
                    c0 = m * D + cb * 512
                    S.dma("pool", sl.rearrange("p (k c) -> p k c", k=16), wmod_v[:, half * 16:(half + 1) * 16, c0:c0 + 512], ss, writes=[sr])
                    for k in range(16):
                        kc = half * 16 + k
                        S.op("pe", lambda e, pb=pb, sl=sl, k=k, kc=kc: e.matmul(pb[0:17, :], cT[:, kc, :], sl[:, k * 512:(k + 1) * 512], start=(kc == 0), stop=(kc == 31)),
                             reads=[res("cT"), sr], writes=[psr[2 + cb % 2]], inc=(k == 15))
                S.op("act", lambda e, pb=pb, cb=cb: e.activation(mrow[0:17, cb * 512:(cb + 1) * 512], pb[0:17, :], AF.Copy),
                     reads=[psr[2 + cb % 2]], writes=[res("mrow")])
            for ty in range(2):
                st = mst[ty]; sr = res("mst%d" % ty)
                for cb in range(8):
                    pb = PS[4 + cb % 2]
                    S.op("pe", lambda e, pb=pb, cb=cb, ty=ty: e.matmul(pb[:, :], sel[:, ty, :], mrow[0:18, cb * 512:(cb + 1) * 512], start=True, stop=True),
                         reads=[res("mrow"), r_const], writes=[psr[4 + cb % 2]])
                    if m in (1, 4):
                        S.op("dve", lambda e, pb=pb, cb=cb, st=st: e.scalar_tensor_tensor(st[:, cb * 512:(cb + 1) * 512], pb[:, :], 1.0, gb[:, cb * 512:(cb + 1) * 512], OP.add, OP.mult),
                             reads=[psr[4 + cb % 2], res("gb")], writes=[sr])
                    else:
                        S.op("dve", lambda e, pb=pb, cb=cb, st=st: e.tensor_copy(st[:, cb * 512:(cb + 1) * 512], pb[:, :]),
                             reads=[psr[4 + cb % 2]], writes=[sr])
                S.dma("sp", modb[ty, m], st[:, :], slot("mst%d" % ty), reads=[sr], writes=[res("modb%d_%d" % (ty, m))])
        P0_names = ["csl", "mrow", "gb", "mst0", "mst1", "slabA0", "slabA1", "slabA2"]

        hT = bv(0, 32 * 640).rearrange("p (k c) -> p k c", k=32)
        kT = bv(40, 4 * 640).rearrange("p (g c) -> p g c", g=4)
        Vt = bv(45, 5 * 256).rearrange("p (t c) -> p t c", t=5)
        PT = [bv(48 + i, 512) for i in range(4)]
        rec = [fv(52, 512), fv(54, 512)]
        cwk = [fv(56 + 2 * i, 512) for i in range(5)]
        aT = bv(0, FC * 384).rearrange("p (f c) -> p f c", f=FC)
        qT = bv(66, 16 * 512).rearrange("p (k c) -> p k c", k=16)
        cvT = bv(82, 16 * 512).rearrange("p (k c) -> p k c", k=16)
        zT = bv(66, 32 * 512).rearrange("p (k c) -> p k c", k=32)
        xt = [fv(98, D), fv(114, D)]
        nt1 = {"tq": [fv(130, 1024), fv(134, 1024)], "gs": [fv(138, 1024), fv(142, 1024)], "sh": [fv(146, 1024), fv(150, 1024)]}
        nt3 = {"tq": [fv(0, 1024), fv(4, 1024)], "gs": [fv(8, 1024), fv(12, 1024)], "sh": [fv(16, 1024), fv(20, 1024)]}
        gpc = [fv(24, 512), fv(26, 512)]
        slab2 = [bv(98, 32 * 256), bv(114, 32 * 256), bv(130, 32 * 256), bv(146, 32 * 256)]
        slab3 = [bv(28, 16 * 512), bv(44, 16 * 512)]
        kcT = bv(146, 8 * 4 * 128).rearrange("p (s g k) -> p s g k", s=8, g=4)
        vcs = bv(154, 8 * 256).rearrange("p (s c) -> p s c", s=8)
        xm = fv(98, 4 * D).rearrange("p (t c) -> p t c", t=4)
        xo = fv(98, 3 * D).rearrange("p (t c) -> p t c", t=3)
        slab5 = [bv(146, 8 * 512), bv(154, 8 * 512), bv(90, 8 * 512)]
        ytmp = fv(66, D)
        pc5 = [fv(82, 512), fv(84, 512), fv(86, 512), fv(88, 512)]
        fwk = [fv(146 + 2 * i, 512) for i in range(6)]

        w_in_v = w_in.rearrange("(kc p) c -> p kc c", p=128)
        w_out_v = w_out.rearrange("(kc p) c -> p kc c", p=128)
        w_gu_v = w_gu.rearrange("(kc p) c -> p kc c", p=128)
        w_dn_v = w_dn.rearrange("(fc p) c -> p fc c", p=128)

        A_names = ["hT", "kT", "Vt", "PT0", "PT1", "PT2", "PT3", "rec0", "rec1", "cwk0", "cwk1", "cwk2", "cwk3", "cwk4",
                   "aT", "nt3", "gpc0", "gpc1", "slab3_0", "slab3_1", "mst1", "slabA0", "slabA1", "slabA2"]
        B_names = ["qT", "cvT", "zT", "ytmp", "pc5_0", "pc5_1", "pc5_2", "pc5_3"]
        C_names = ["csl", "mrow", "gb", "mst0", "xt0", "xt1", "nt1", "slab2_0", "slab2_1", "slab2_2", "kcT", "vcs", "xm0", "xm1", "xm2", "xm3",
                   "xo0", "xo1", "xo2", "slab5_0", "slab5_1", "fwk"]

        slab_ctr = {"s2": 0, "s3": 0, "s5": 0}

        def norm_tile(src, src_res, ty, m_scale, m_shift, ntb, dstT, dst_res, col0, tmp_pref):
            st = stats
            S.op("act", lambda e: e.activation(ntb["tq"][0][:, :], src[:, 0:1024], AF.Square, accum_out=st[:, 0:1]),
                 reads=[src_res], writes=[res(tmp_pref + "tq0"), res("st0")])
            for q in range(1, 4):
                S.op("act", lambda e, q=q: e.activation(ntb["tq"][0][:, :], src[:, q * 1024:(q + 1) * 1024], AF.Square, accum_out=st[:, q:q + 1]),
                     reads=[src_res], writes=[res(tmp_pref + "tq0"), res("st%d" % q)])
            S.op("dve", lambda e: e.tensor_reduce(st[:, 4:5], st[:, 0:4], mybir.AxisListType.X, OP.add),
                 reads=[res("st0"), res("st1"), res("st2"), res("st3")], writes=[res("st4")])
            S.op("dve", lambda e: e.tensor_scalar(st[:, 5:6], st[:, 4:5], 1.0 / D, EPS, OP.mult, OP.add), reads=[res("st4")], writes=[res("st5")])
            S.op("act", lambda e: e.activation(st[:, 6:7], st[:, 5:6], AF.Sqrt), reads=[res("st5")], writes=[res("st6")])
            S.op("dve", lambda e: e.reciprocal(st[:, 7:8], st[:, 6:7]), reads=[res("st6")], writes=[res("st7")])
            for q in range(4):
                b = q % 2
                gs = ntb["gs"][b]; sh = ntb["sh"][b]; tq = ntb["tq"][b]
                rg = res(tmp_pref + "gs%d" % b); rs = res(tmp_pref + "sh%d" % b); rt = res(tmp_pref + "tq%d" % b)
                S.dma("sp", gs[:, :], modb[ty, m_scale][:, q * 1024:(q + 1) * 1024], slot(tmp_pref + "gs%d" % b),
                      reads=[res("modb%d_%d" % (ty, m_scale))], writes=[rg])
                S.dma("sp", sh[:, :], modb[ty, m_shift][:, q * 1024:(q + 1) * 1024], slot(tmp_pref + "sh%d" % b),
                      reads=[res("modb%d_%d" % (ty, m_shift))], writes=[rs])
                S.op("dve", lambda e, q=q, gs=gs, tq=tq: e.scalar_tensor_tensor(tq[:, :], src[:, q * 1024:(q + 1) * 1024], st[:, 7:8], gs[:, :], OP.mult, OP.mult),
                     reads=[src_res, res("st7"), rg], writes=[rt])
                S.op("dve", lambda e, tq=tq, sh=sh: e.tensor_tensor(tq[:, :], tq[:, :], sh[:, :], OP.add), reads=[rt, rs], writes=[rt])
                for h2 in range(2):
                    pb = PS[6 + h2]
                    for j in range(4):
                        S.op("pe", lambda e, pb=pb, j=j, tq=tq, h2=h2: e.transpose(pb[:, j * 128:(j + 1) * 128], tq[:, (h2 * 4 + j) * 128:(h2 * 4 + j + 1) * 128], ident[:, :]),
                             reads=[rt, r_const], writes=[psr[6 + h2]], inc=(j == 3))
                    kc0 = q * 8 + h2 * 4
                    eng = "act" if h2 == 0 else "dve"
                    if eng == "act":
                        S.op("act", lambda e, pb=pb, kc0=kc0: e.activation(dstT[:, kc0:kc0 + 4, col0:col0 + 128], pb[:, :].rearrange("p (a b) -> p a b", a=4), AF.Copy),
                             reads=[psr[6 + h2]], writes=[dst_res])
                    else:
                        S.op("dve", lambda e, pb=pb, kc0=kc0: e.tensor_copy(dstT[:, kc0:kc0 + 4, col0:col0 + 128], pb[:, :].rearrange("p (a b) -> p a b", a=4)),
                             reads=[psr[6 + h2]], writes=[dst_res])

        out_slots = []
        groups = [
            dict(kv=0, tiles=[("p", 128), ("p", 256), ("p", 384), ("p", 512)], own0=1, ty=0),
            dict(kv=512, tiles=[("p", 640), ("p", 768), ("p", 896)], own0=0, ty=0),
            dict(kv=896, tiles=[("p", 1024), ("p", 1152), ("s", 0)], own0=0, ty=1),
        ]
        xmid_idx = 0
        def stage(n):
            if n >= K_STOP:
                raise _StopBuild()
        try:
          stage(1)
          for gi, G in enumerate(groups):
              if str(gi) not in os.environ.get("K_GROUPS", "012"):
                  continue
              tiles = G["tiles"]; NT = len(tiles); NP = NT * 128
              own0 = G["own0"]; NOWN = NT - own0
              has_s = tiles[-1][0] == "s"
              NPP = NP - (128 if has_s else 0)
              fence_all(["hT", "xt0", "xt1", "nt1tq0", "nt1tq1", "nt1gs0", "nt1gs1", "nt1sh0", "nt1sh1", "kT", "Vt"])
              alltiles = [("p", G["kv"])] + tiles
              for ti, (kind, row) in enumerate(alltiles):
                  b = ti % 2
                  srcd = xp[row:row + 128, :] if kind == "p" else xs[:, :]
                  S.dma("sp", xt[b][:, :], srcd, slot("xt%d" % b), writes=[res("xt%d" % b)])
                  norm_tile(xt[b], res("xt%d" % b), 1 if kind == "s" else 0, 1, 0, nt1, hT, res("hT"), ti * 128, "nt1")
              stage(2 + 10 * gi)
              fence_all(["slab2_0", "slab2_1", "slab2_2", "qT", "cvT", "PT0", "PT1", "PT2", "PT3", "rec0", "rec1",
                         "cwk0", "cwk1", "cwk2", "cwk3", "cwk4", "kcT", "vcs"])
              CT = 128 + NP

              def load_slab2(c0, src=w_in_v, nsl=3):
                  i = slab_ctr["s2"] % nsl; slab_ctr["s2"] += 1
                  sl = slab2[i].rearrange("p (k c) -> p k c", k=32)
                  S.dma("pool", sl, src[:, :, c0:c0 + 256], slot("slab2_%d" % i), writes=[res("slab2_%d" % i)])
                  return sl, res("slab2_%d" % i)

              slK, rK = load_slab2(2048)
              for g in range(4):
                  for (c0, cn, pbi) in ((0, min(512, CT), 0), (512, CT - 512, 1)):
                      if cn <= 0:
                          continue
                      pb = PS[pbi]
                      for half in range(2):
                          for kc in range(KC):
                              S.op("pe", lambda e, pb=pb, half=half, kc=kc, g=g, c0=c0, cn=cn: e.matmul(
                                  pb[half * 64:(half + 1) * 64, 0:cn], slK[:, kc, g * 64:(g + 1) * 64], hT[:, kc, c0:c0 + cn],
                                  start=(kc == 0), stop=(kc == KC - 1)), reads=[res("hT"), rK], writes=[psr[pbi]], inc=(half == 1 and kc == KC - 1))
                      S.op("act", lambda e, pb=pb, g=g, c0=c0, cn=cn: e.activation(kT[:, g, c0:c0 + cn], pb[:, 0:cn], AF.Copy),
                           reads=[psr[pbi]], writes=[res("kT")])
              slV, rV = load_slab2(2304)
              for ti in range(NT + 1):
                  pb = PS[2 + ti % 2]
                  for kc in range(KC):
                      S.op("pe", lambda e, pb=pb, kc=kc, ti=ti: e.matmul(pb[:, 0:256], hT[:, kc, ti * 128:(ti + 1) * 128], slV[:, kc, :], start=(kc == 0), stop=(kc == KC - 1)),
                           reads=[res("hT"), rV], writes=[psr[2 + ti % 2]], inc=(kc == KC - 1))
                  S.op("act", lambda e, pb=pb, ti=ti: e.activation(Vt[:, ti, :], pb[:, 0:256], AF.Copy), reads=[psr[2 + ti % 2]], writes=[res("Vt")])
                  is_out = (gi == 2 and ti in (2, 3) and "o" not in os.environ.get("K_SKIP", ""))
                  if is_out:
                      oi = ti - 2
                      S.op("dve", lambda e, pb=pb, oi=oi: e.tensor_copy(kvo[:, oi, 256:512], pb[:, 0:256]), reads=[psr[2 + ti % 2]], writes=[res("kvo%d" % oi)])
                      pk = PS[4 + oi]
                      for kc in range(KC):
                          S.op("pe", lambda e, pk=pk, kc=kc, ti=ti: e.matmul(pk[:, 0:256], hT[:, kc, ti * 128:(ti + 1) * 128], slK[:, kc, :], start=(kc == 0), stop=(kc == KC - 1)),
                               reads=[res("hT"), rK], writes=[psr[4 + oi]], inc=(kc == KC - 1))
                      S.op("dve", lambda e, pk=pk, oi=oi: e.tensor_copy(kvo[:, oi, 0:256], pk[:, 0:256]), reads=[psr[4 + oi]], writes=[res("kvo%d" % oi)])
              KS = os.environ.get("K_SKIP", "")
              if gi == 2 and "k" not in KS:
                  so = slot("kvout"); out_slots.append(so)
                  S.dma("sp", kwin_p[:, :], kvo[:, 0, 0:256], so, reads=[res("kvo0")])
                  S.dma("sp", vwin_p[:, :], kvo[:, 0, 256:512], so, reads=[res("kvo0")])
                  for sq in range(16):
                      S.dma("sp", kwin_s[sq, 120:128, :], kvo[sq * 8:(sq + 1) * 8, 1, 0:256], so, reads=[res("kvo1")])
                      S.dma("sp", vwin_s[sq, 120:128, :], kvo[sq * 8:(sq + 1) * 8, 1, 256:512], so, reads=[res("kvo1")])
                  if not os.environ.get("K_NOD2D"):
                      S.dma("sp", kwin_s[:, 0:120, :], kc_d[:, 8:128, :], so)
                      S.dma("sp", vwin_s[:, 0:120, :], vc_d[:, 8:128, :], so)
              for sq in range(8):
                  sl, rs_ = load_slab2(sq * 256)
                  for c in range(2):
                      ch = sq * 2 + c
                      pb = PS[ch % 2]
                      for kc in range(KC):
                          S.op("pe", lambda e, pb=pb, kc=kc, c=c, sl=sl: e.matmul(pb[:, 0:NP], sl[:, kc, c * 128:(c + 1) * 128], hT[:, kc, 128:128 + NP], start=(kc == 0), stop=(kc == KC - 1)),
                               reads=[res("hT"), rs_], writes=[psr[ch % 2]], inc=(kc == KC - 1))
                      S.op("act", lambda e, pb=pb, ch=ch: e.activation(qT[:, ch, 0:NP], pb[:, 0:NP], AF.Copy), reads=[psr[ch % 2]], writes=[res("qT")])
              for cj in range(8):
                  slB, rB = load_slab2(2560 + cj * 256)
                  slC, rC = load_slab2(4608 + cj * 256)
                  slH, rH = load_slab2(6656 + cj * 256)
                  for c in range(2):
                      ch = cj * 2 + c
                      base = 2 + 3 * (ch % 2) if False else (2 if ch % 2 == 0 else 5)
                      bks = (2, 3, 4) if ch % 2 == 0 else (5, 0, 1)
                      for (sl, rr, bk) in ((slB, rB, bks[0]), (slC, rC, bks[1]), (slH, rH, bks[2])):
                          for kc in range(KC):
                              S.op("pe", lambda e, bk=bk, kc=kc, c=c, sl=sl: e.matmul(PS[bk][:, 0:NP], sl[:, kc, c * 128:(c + 1) * 128], hT[:, kc, 128:128 + NP], start=(kc == 0), stop=(kc == KC - 1)),
                                   reads=[res("hT"), rr], writes=[psr[bk]], inc=(kc == KC - 1))
                      pB, pC, pH = PS[bks[0]], PS[bks[1]], PS[bks[2]]
                      hcs, ub, t1, t2 = cwk[0], cwk[1], cwk[2], cwk[3]
                      S.op("act", lambda e, pH=pH: e.activation(hcs[:, 0:NP], pH[:, 0:NP], AF.Copy), reads=[psr[bks[2]]], writes=[res("cwk0")])
                      if NPP > 0:
                          S.op("dve", lambda e, ch=ch: e.tensor_copy(wk[0][:, 0:2], ucarry[:, ch, :]), reads=[res("ucarry")], writes=[wkr[0]])
                          S.op("dve", lambda e, pC=pC: e.tensor_tensor(wk[0][:, 2:2 + NPP], pC[:, 0:NPP], hcs[:, 0:NPP], OP.mult),
                               reads=[psr[bks[1]], res("cwk0")], writes=[wkr[0]])
                          if gi == 0:
                              S.op("dve", lambda e: e.tensor_scalar(wk[0][:, 2:130], wk[0][:, 2:130], flag[:, 0:1], None, OP.mult), reads=[wkr[0], r_const], writes=[wkr[0]])
                          S.op("act", lambda e, ch=ch: e.activation(ucarry[:, ch, :], wk[0][:, NPP:NPP + 2], AF.Copy), reads=[wkr[0]], writes=[res("ucarry")])
                          S.op("dve", lambda e, ch=ch: e.tensor_scalar(t1[:, 0:NPP], wk[0][:, 0:NPP], cw[:, ch, 0:1], None, OP.mult), reads=[wkr[0], r_const], writes=[res("cwk2")])
                          S.op("dve", lambda e, ch=ch: e.scalar_tensor_tensor(t2[:, 0:NPP], wk[0][:, 1:1 + NPP], cw[:, ch, 1:2], t1[:, 0:NPP], OP.mult, OP.add), reads=[wkr[0], res("cwk2")], writes=[res("cwk3")])
                          S.op("dve", lambda e, ch=ch: e.scalar_tensor_tensor(t1[:, 0:NPP], wk[0][:, 2:2 + NPP], cw[:, ch, 2:3], t2[:, 0:NPP], OP.mult, OP.add), reads=[wkr[0], res("cwk3")], writes=[res("cwk2")])
                          S.op("dve", lambda e, ch=ch, pB=pB: e.tensor_tensor(cvT[:, ch, 0:NPP], pB[:, 0:NPP], t1[:, 0:NPP], OP.mult), reads=[psr[bks[0]], res("cwk2")], writes=[res("cvT")])
                      if has_s and "s" not in KS:
                          s0 = NPP
                          u3 = wk[1][:, 0:160].rearrange("p (s t) -> p s t", s=16)
                          S.op("dve", lambda e, ch=ch: e.tensor_copy(u3[:, :, 0:2], ush[:, ch, :, :]), reads=[r_const], writes=[wkr[1]])
                          S.op("dve", lambda e, pC=pC: e.tensor_tensor(u3[:, :, 2:10], pC[:, s0:s0 + 128].rearrange("p (s t) -> p s t", s=16), hcs[:, s0:s0 + 128].rearrange("p (s t) -> p s t", s=16), OP.mult),
                               reads=[psr[bks[1]], res("cwk0")], writes=[wkr[1]])
                          S.op("act", lambda e, ch=ch: e.activation(usl[:, ch, :, :], u3[:, :, 8:10], AF.Copy), reads=[wkr[1]], writes=[res("usl")])
                          t13 = wk[2][:, 0:128].rearrange("p (s t) -> p s t", s=16); t23 = wk[3][:, 0:128].rearrange("p (s t) -> p s t", s=16)
                          S.op("dve", lambda e, ch=ch: e.tensor_scalar(t13, u3[:, :, 0:8], cw[:, ch, 0:1], None, OP.mult), reads=[wkr[1], r_const], writes=[wkr[2]])
                          S.op("dve", lambda e, ch=ch: e.scalar_tensor_tensor(t23, u3[:, :, 1:9], cw[:, ch, 1:2], t13, OP.mult, OP.add), reads=[wkr[1], wkr[2]], writes=[wkr[3]])
                          S.op("dve", lambda e, ch=ch: e.scalar_tensor_tensor(t13, u3[:, :, 2:10], cw[:, ch, 2:3], t23, OP.mult, OP.add), reads=[wkr[1], wkr[3]], writes=[wkr[2]])
                          S.op("dve", lambda e, ch=ch, pB=pB: e.tensor_tensor(cvT[:, ch, s0:s0 + 128], pB[:, s0:s0 + 128], wk[2][:, 0:128], OP.mult), reads=[psr[bks[0]], wkr[2]], writes=[res("cvT")])
              if gi == 2 and "c" not in KS:
                  so = slot("convout"); out_slots.append(so)
                  S.dma("sp", conv_p, ucarry[:], so, reads=[res("ucarry")])
                  S.dma("sp", conv_s, usl[:], so, reads=[res("usl")])
              stage(3 + 10 * gi)
              for tj in range(NT):
                  kind = tiles[tj][0]
                  qc0 = tj * 128
                  kcur = 128 + tj * 128
                  kprev = tj * 128
                  if kind == "p":
                      mcur = 0
                      mprev = 2 if (gi == 0 and tj == 1) else 1
                  else:
                      mcur = 3
                  for g in range(4):
                      blks = ("prev", "cur") if kind == "p" else ("cur",)
                      pts = {}
                      bi = 0
                      for blk in blks:
                          kcol = kprev if blk == "prev" else kcur
                          for par in range(2):
                              pb = PS[bi]; pt = PT[bi]
                              for jh in range(4):
                                  chq = 4 * g + jh
                                  S.op("pe", lambda e, pb=pb, par=par, jh=jh, chq=chq, kcol=kcol, g=g: e.matmul(
                                      pb[:, jh * 128:(jh + 1) * 128], kT[par * 64:(par + 1) * 64, g, kcol:kcol + 128], qT[par * 64:(par + 1) * 64, chq, qc0:qc0 + 128], start=True, stop=True),
                                      reads=[res("kT"), res("qT")], writes=[psr[bi]], inc=(jh == 3))
                              S.op("act", lambda e, pb=pb, pt=pt: e.activation(pt[:, :], pb[:, :], AF.Exp, scale=0.125), reads=[psr[bi]], writes=[res("PT%d" % bi)])
                              mi = (mprev if blk == "prev" else mcur)
                              S.op("dve", lambda e, pt=pt, mi=mi: e.tensor_tensor(pt[:, :], pt[:, :], masks[:, mi, :], OP.mult), reads=[res("PT%d" % bi), res("masks")], writes=[res("PT%d" % bi)])
                              pts[(blk, par)] = bi
                              bi += 1
                      pO, pD = PS[4], PS[5]
                      if kind == "p":
                          for par in range(2):
                              for (pbk, is_o) in ((pO, True), (pD, False)):
                                  for n_, blk in enumerate(blks):
                                      vti = tj if blk == "prev" else tj + 1
                                      bi = pts[(blk, par)]
                                      lhs = (lambda vti=vti, g=g: Vt[:, vti, g * 64:(g + 1) * 64]) if is_o else (lambda: onesb[:, :])
                                      S.op("pe", lambda e, pbk=pbk, par=par, lhs=lhs, bi=bi, n_=n_: e.matmul(pbk[par * 64:(par + 1) * 64, :], lhs(), PT[bi][:, :], start=(n_ == 0), stop=(n_ == 1)),
                                           reads=[res("Vt"), res("PT%d" % bi), res("ones")], writes=[psr[4 if is_o else 5]], inc=(n_ == 1))
                      else:
                          for hs in range(2):
                              S.dma("pool", kcT, kcT_d[:, hs * 8:(hs + 1) * 8, :, :], slot("kcT"), writes=[res("kcT")])
                              S.dma("pool", vcs, vc_d[hs * 8:(hs + 1) * 8, :, :].rearrange("s p c -> p s c"), slot("vcs"), writes=[res("vcs")])
                              for par in range(2):
                                  pb = PS[2 + par]; pt = PT[2 + par]
                                  for s8 in range(8):
                                      sq = hs * 8 + s8
                                      for jh in range(4):
                                          S.op("pe", lambda e: e.matmul(
                                              pb[:, s8 * 32 + jh * 8:s8 * 32 + jh * 8 + 8], kcT[par * 64:(par + 1) * 64, s8, g, :], qT[par * 64:(par + 1) * 64, 4 * g + jh, qc0 + sq * 8:qc0 + sq * 8 + 8], start=True, stop=True),
                                              reads=[res("kcT"), res("qT")], writes=[psr[2 + par]], inc=(s8 == 7 and jh == 3))
                                  S.op("act", lambda e, pb=pb, pt=pt: e.activation(pt[:, 0:256], pb[:, 0:256], AF.Exp, scale=0.125), reads=[psr[2 + par]], writes=[res("PT%d" % (2 + par))])
                                  S.op("dve", lambda e, pt=pt: e.tensor_tensor(pt[:, 0:256], pt[:, 0:256], masks[:, 4, 0:256], OP.mult), reads=[res("PT%d" % (2 + par)), res("masks")], writes=[res("PT%d" % (2 + par))])
                                  for s8 in range(8):
                                      sq = hs * 8 + s8
                                      last = (hs == 1 and s8 == 7)
                                      for jh in range(4):
                                          oc = jh * 128 + sq * 8
                                          pc = s8 * 32 + jh * 8
                                          fst = (hs == 0 and s8 == 0 and jh == 0)
                                          S.op("pe", lambda e: e.matmul(pO[par * 64:(par + 1) * 64, oc:oc + 8], vcs[:, s8, g * 64:(g + 1) * 64], pt[:, pc:pc + 8], start=fst, stop=False, skip_group_check=True),
                                               reads=[res("vcs"), res("PT%d" % (2 + par))], writes=[psr[4]], inc=(s8 == 7 and jh == 3))
                                          S.op("pe", lambda e: e.matmul(pD[par * 64:(par + 1) * 64, oc:oc + 8], onesb[:, :], pt[:, pc:pc + 8], start=fst, stop=False, skip_group_check=True),
                                               reads=[res("ones"), res("PT%d" % (2 + par))], writes=[psr[5]], inc=(s8 == 7 and jh == 3))
                      if kind != "p":
                          for par in range(2):
                              bi = pts[("cur", par)]
                              S.op("pe", lambda e: e.matmul(pO[par * 64:(par + 1) * 64, :], Vt[:, tj + 1, g * 64:(g + 1) * 64], PT[bi][:, :], start=False, stop=True, skip_group_check=True),
                                   reads=[res("Vt"), res("PT%d" % bi)], writes=[psr[4]])
                              S.op("pe", lambda e: e.matmul(pD[par * 64:(par + 1) * 64, :], onesb[:, :], PT[bi][:, :], start=False, stop=True, skip_group_check=True),
                                   reads=[res("ones"), res("PT%d" % bi)], writes=[psr[5]])
                      S.op("dve", lambda e, g=g: e.tensor_tensor(rec[0][:, :].rearrange("p (j q) -> p j q", j=4), pD[:, :].rearrange("p (j q) -> p j q", j=4),
                                                               esk[:, 4 * g:4 * g + 4].unsqueeze(2).to_broadcast([128, 4, 128]), OP.add),
                           reads=[psr[5], res("esk")], writes=[res("rec0")])
                      S.op("dve", lambda e: e.reciprocal(rec[1][:, :], rec[0][:, :]), reads=[res("rec0")], writes=[res("rec1")])
                      S.op("dve", lambda e, g=g, qc0=qc0: e.tensor_tensor(qT[:, 4 * g:4 * g + 4, qc0:qc0 + 128], pO[:, :].rearrange("p (j q) -> p j q", j=4), rec[1][:, :].rearrange("p (j q) -> p j q", j=4), OP.mult),
                           reads=[psr[4], res("rec1")], writes=[res("qT")])
              stage(4 + 10 * gi)
              fence_all(["xm0", "xm1", "xm2", "xm3", "slab3_0", "slab3_1", "gpc0", "gpc1",
                         "nt3tq0", "nt3tq1", "nt3gs0", "nt3gs1", "nt3sh0", "nt3sh1"])
              for tj, (kind, row) in enumerate(tiles):
                  srcd = xp[row:row + 128, :] if kind == "p" else xs[:, :]
                  S.dma("sp", xm[:, tj, :], srcd, slot("xm%d" % tj), writes=[res("xm%d" % tj)])
              for cb in range(8):
                  sls = []
                  for half in range(2):
                      i = slab_ctr["s3"] % 2; slab_ctr["s3"] += 1
                      sl = slab3[i].rearrange("p (k c) -> p k c", k=16)
                      S.dma("pool", sl, w_out_v[:, half * 16:(half + 1) * 16, cb * 512:(cb + 1) * 512], slot("slab3_%d" % i), writes=[res("slab3_%d" % i)])
                      for tj in range(NT):
                          for k in range(16):
                              kc = half * 16 + k
                              src = qT if kc < 16 else cvT
                              S.op("pe", lambda e, tj=tj, kc=kc, k=k, sl=sl, src=src: e.matmul(PS[tj][:, :], src[:, kc % 16, tj * 128:(tj + 1) * 128], sl[:, k, :], start=(kc == 0), stop=(kc == KC - 1)),
                                   reads=[res("qT"), res("cvT"), res("slab3_%d" % i)], writes=[psr[tj]], inc=(k == 15))
                  for ty in sorted(set(1 if t[0] == "s" else 0 for t in tiles)):
                      gi_ = ty
                      S.dma("sp", gpc[gi_][:, :], modb[ty, 2][:, cb * 512:(cb + 1) * 512], slot("gpc%d" % gi_), reads=[res("modb%d_2" % ty)], writes=[res("gpc%d" % gi_)])
                  for tj, (kind, row) in enumerate(tiles):
                      ty = 1 if kind == "s" else 0
                      S.op("dve", lambda e, tj=tj, ty=ty: e.tensor_tensor(wk[4][:, 0:512], PS[tj][:, :], gpc[ty][:, :], OP.mult), reads=[psr[tj], res("gpc%d" % ty)], writes=[wkr[4]])
                      S.op("dve", lambda e, tj=tj, cb=cb: e.tensor_tensor(xm[:, tj, cb * 512:(cb + 1) * 512], xm[:, tj, cb * 512:(cb + 1) * 512], wk[4][:, 0:512], OP.add),
                           reads=[wkr[4], res("xm%d" % tj)], writes=[res("xm%d" % tj)])
              fence(["qT", "cvT"], ["zT"])
              xmid_of = {}
              for tj, (kind, row) in enumerate(tiles):
                  ty = 1 if kind == "s" else 0
                  norm_tile(xm[:, tj, :], res("xm%d" % tj), ty, 4, 3, nt3, zT, res("zT"), tj * 128, "nt3")
                  if tj >= own0:
                      S.dma("sp", xmid[xmid_idx], xm[:, tj, :], slot("xmsp%d" % tj), reads=[res("xm%d" % tj)], writes=[res("xmid%d" % xmid_idx)])
                      xmid_of[tj] = xmid_idx
                      xmid_idx += 1
              stage(5 + 10 * gi)
              fence_all(["aT", "slab2_0", "slab2_1", "slab2_2", "fwk"])
              OC0 = own0 * 128; NO = NOWN * 128
              for fj in range(FC // 2):
                  slG, rG = load_slab2(fj * 256, w_gu_v, 4)
                  slU, rU = load_slab2(DFF + fj * 256, w_gu_v, 4)
                  for c in range(2):
                      fc = fj * 2 + c
                      pG = PS[fc % 2]; pU = PS[2 + fc % 2]
                      for kc in range(KC):
                          S.op("pe", lambda e, pG=pG, kc=kc, c=c: e.matmul(pG[:, 0:NP], slG[:, kc, c * 128:(c + 1) * 128], zT[:, kc, 0:NP], start=(kc == 0), stop=(kc == KC - 1)),
                               reads=[res("zT"), rG], writes=[psr[fc % 2]], inc=(kc == KC - 1))
                      for kc in range(KC):
                          S.op("pe", lambda e, pU=pU, kc=kc, c=c: e.matmul(pU[:, 0:NO], slU[:, kc, c * 128:(c + 1) * 128], zT[:, kc, OC0:OC0 + NO], start=(kc == 0), stop=(kc == KC - 1)),
                               reads=[res("zT"), rU], writes=[psr[2 + fc % 2]], inc=(kc == KC - 1))
                      gp, t1, t2 = wk[0], wk[2], wk[3]
                      if NPP > 0:
                          S.op("act", lambda e, fc=fc: e.activation(gp[:, 0:2], gcarry[:, fc, :], AF.Copy), reads=[res("gcarry")], writes=[wkr[0]])
                          S.op("act", lambda e, pG=pG: e.activation(gp[:, 2:2 + NPP], pG[:, 0:NPP], AF.Copy), reads=[psr[fc % 2]], writes=[wkr[0]])
                          if gi == 0:
                              S.op("dve", lambda e: e.tensor_scalar(gp[:, 2:130], gp[:, 2:130], flag[:, 0:1], None, OP.mult), reads=[wkr[0], r_const], writes=[wkr[0]])
                          S.op("act", lambda e, fc=fc: e.activation(gcarry[:, fc, :], gp[:, NPP:NPP + 2], AF.Copy), reads=[wkr[0]], writes=[res("gcarry")])
                          S.op("dve", lambda e, fc=fc: e.tensor_scalar(t1[:, 0:NPP], gp[:, 0:NPP], fcw[:, fc, 0:1], None, OP.mult), reads=[wkr[0], r_const], writes=[wkr[2]])
                          S.op("dve", lambda e, fc=fc: e.scalar_tensor_tensor(t2[:, 0:NPP], gp[:, 1:1 + NPP], fcw[:, fc, 1:2], t1[:, 0:NPP], OP.mult, OP.add), reads=[wkr[0], wkr[2]], writes=[wkr[3]])
                          S.op("dve", lambda e, fc=fc: e.scalar_tensor_tensor(t1[:, 0:NPP], gp[:, 2:2 + NPP], fcw[:, fc, 2:3], t2[:, 0:NPP], OP.mult, OP.add), reads=[wkr[0], wkr[3]], writes=[wkr[2]])
                          S.op("act", lambda e: e.activation(t2[:, 0:NPP], t1[:, 0:NPP], AF.Silu), reads=[wkr[2]], writes=[wkr[3]])
                          NPO = NPP - OC0
                          S.op("dve", lambda e, fc=fc, pU=pU, NPO=NPO: e.tensor_tensor(aT[:, fc, 0:NPO], pU[:, 0:NPO], t2[:, OC0:OC0 + NPO], OP.mult), reads=[psr[2 + fc % 2], wkr[3]], writes=[res("aT")])
                      if has_s:
                          s0 = NPP
                          g3 = wk[1][:, 0:160].rearrange("p (s t) -> p s t", s=16)
                          S.dma("sp", gq[:, fc % 2, :, :], gph_d[:, fc, :, :], slot("gq%d" % (fc % 2)), writes=[res("gq%d" % (fc % 2))])
                          S.op("act", lambda e, fc=fc: e.activation(g3[:, :, 0:2], gq[:, fc % 2, :, :], AF.Copy), reads=[res("gq%d" % (fc % 2))], writes=[wkr[1]])
                          S.op("act", lambda e, pG=pG: e.activation(g3[:, :, 2:10], pG[:, s0:s0 + 128].rearrange("p (s t) -> p s t", s=16), AF.Copy), reads=[psr[fc % 2]], writes=[wkr[1]])
                          S.op("act", lambda e, fc=fc: e.activation(gso[:, fc % 2, :, :], g3[:, :, 8:10], AF.Copy), reads=[wkr[1]], writes=[res("gso%d" % (fc % 2))])
                          S.dma("sp", ffn_s[:, fc, :, :], gso[:, fc % 2, :, :], slot("gso%d" % (fc % 2)), reads=[res("gso%d" % (fc % 2))])
                          t13 = wk[4][:, 0:128].rearrange("p (s t) -> p s t", s=16); t23 = wk[5][:, 0:128].rearrange("p (s t) -> p s t", s=16)
                          S.op("dve", lambda e, fc=fc: e.tensor_scalar(t13, g3[:, :, 0:8], fcw[:, fc, 0:1], None, OP.mult), reads=[wkr[1], r_const], writes=[wkr[4]])
                          S.op("dve", lambda e, fc=fc: e.scalar_tensor_tensor(t23, g3[:, :, 1:9], fcw[:, fc, 1:2], t13, OP.mult, OP.add), reads=[wkr[1], wkr[4]], writes=[wkr[5]])
                          S.op("dve", lambda e, fc=fc: e.scalar_tensor_tensor(t13, g3[:, :, 2:10], fcw[:, fc, 2:3], t23, OP.mult, OP.add), reads=[wkr[1], wkr[5]], writes=[wkr[4]])
                          S.op("act", lambda e: e.activation(wk[5][:, 0:128], wk[4][:, 0:128], AF.Silu), reads=[wkr[4]], writes=[wkr[5]])
                          S.op("dve", lambda e, fc=fc, pU=pU: e.tensor_tensor(aT[:, fc, s0 - OC0:s0 - OC0 + 128], pU[:, s0 - OC0:s0 - OC0 + 128], wk[5][:, 0:128], OP.mult), reads=[psr[2 + fc % 2], wkr[5]], writes=[res("aT")])
              if gi == 2:
                  so = slot("ffnout"); out_slots.append(so)
                  S.dma("sp", ffn_p, gcarry[:], so, reads=[res("gcarry")])
                  out_slots.append(slot("gso0")); out_slots.append(slot("gso1"))
              stage(6 + 10 * gi)
              fence_all(["xo0", "xo1", "xo2", "slab5_0", "slab5_1", "ytmp", "pc5_0", "pc5_1", "pc5_2", "pc5_3"])
              for oj in range(NOWN):
                  tj = own0 + oj
                  S.dma("sp", xo[:, oj, :], xmid[xmid_of[tj]], slot("xo%d" % oj), reads=[res("xmid%d" % xmid_of[tj])], writes=[res("xo%d" % oj)])
              pieces = [(p0, min(8, FC - p0)) for p0 in range(0, FC, 8)]
              for cb in range(8):
                  bank0 = 3 * (cb % 2)
                  for (p0, pn) in pieces:
                      i = slab_ctr["s5"] % 3; slab_ctr["s5"] += 1
                      sl = slab5[i].rearrange("p (k c) -> p k c", k=8)
                      S.dma("pool", sl[:, 0:pn, :], w_dn_v[:, p0:p0 + pn, cb * 512:(cb + 1) * 512], slot("slab5_%d" % i), writes=[res("slab5_%d" % i)])
                      for oj in range(NOWN):
                          for k in range(pn):
                              fc = p0 + k
                              S.op("pe", lambda e, oj=oj, fc=fc, k=k, sl=sl, bank0=bank0: e.matmul(PS[bank0 + oj][:, :], aT[:, fc, oj * 128:(oj + 1) * 128], sl[:, k, :], start=(fc == 0), stop=(fc == FC - 1)),
                                   reads=[res("aT"), res("slab5_%d" % i)], writes=[psr[bank0 + oj]], inc=(k == pn - 1))
                  for ty in sorted(set(1 if t[0] == "s" else 0 for t in tiles[own0:])):
                      S.dma("sp", pc5[ty][:, :], modb[ty, 5][:, cb * 512:(cb + 1) * 512], slot("pc5_%d" % ty), reads=[res("modb%d_5" % ty)], writes=[res("pc5_%d" % ty)])
                  for oj in range(NOWN):
                      ty = 1 if tiles[own0 + oj][0] == "s" else 0
                      S.op("dve", lambda e, oj=oj, ty=ty, bank0=bank0: e.tensor_tensor(wk[4][:, 0:512], PS[bank0 + oj][:, :], pc5[ty][:, :], OP.mult), reads=[psr[bank0 + oj], res("pc5_%d" % ty)], writes=[wkr[4]])
                      S.op("dve", lambda e, oj=oj, cb=cb: e.tensor_tensor(xo[:, oj, cb * 512:(cb + 1) * 512], xo[:, oj, cb * 512:(cb + 1) * 512], wk[4][:, 0:512], OP.add),
                           reads=[wkr[4], res("xo%d" % oj)], writes=[res("xo%d" % oj)])
              for oj in range(NOWN):
                  kind, row = tiles[own0 + oj]
                  src = xo[:, oj, :]; rsrc = res("xo%d" % oj)
                  st = stats
                  for q in range(4):
                      S.op("act", lambda e, q=q, src=src: e.activation(ytmp[:, 0:1024], src[:, q * 1024:(q + 1) * 1024], AF.Square, accum_out=st[:, q:q + 1]),
                           reads=[rsrc], writes=[res("ytmp"), res("st%d" % q)])
                  S.op("dve", lambda e: e.tensor_reduce(st[:, 4:5], st[:, 0:4], mybir.AxisListType.X, OP.add), reads=[res("st0"), res("st1"), res("st2"), res("st3")], writes=[res("st4")])
                  S.op("dve", lambda e: e.tensor_scalar(st[:, 5:6], st[:, 4:5], 1.0 / D, EPS, OP.mult, OP.add), reads=[res("st4")], writes=[res("st5")])
                  S.op("act", lambda e: e.activation(st[:, 6:7], st[:, 5:6], AF.Sqrt), reads=[res("st5")], writes=[res("st6")])
                  S.op("dve", lambda e: e.reciprocal(st[:, 7:8], st[:, 6:7]), reads=[res("st6")], writes=[res("st7")])
                  for q in range(8):
                      b = 2 + q % 2
                      S.dma("sp", pc5[b][:, :], g_fin[q * 512:(q + 1) * 512].partition_broadcast(128), slot("pc5_%d" % b), writes=[res("pc5_%d" % b)])
                      S.op("dve", lambda e, q=q, b=b, src=src: e.scalar_tensor_tensor(ytmp[:, q * 512:(q + 1) * 512], src[:, q * 512:(q + 1) * 512], st[:, 7:8], pc5[b][:, :], OP.mult, OP.mult),
                           reads=[rsrc, res("st7"), res("pc5_%d" % b)], writes=[res("ytmp")])
                  so = slot("yout");
                  if so not in out_slots:
                      out_slots.append(so)
                  dst = y_s[:, :] if kind == "s" else y_p[row - 256:row - 256 + 128, :]
                  S.dma("sp", dst, ytmp[:, :], so, reads=[res("ytmp")])
        except _StopBuild:
            out_slots = list(slots.values())
        S.wait_all("sp", out_slots)

        with nc.Block() as block:
            @block.tensor
            def _(e):
                for c in S.streams["pe"]:
                    c(e)

            @block.scalar
            def _(e):
                for c in S.streams["act"]:
                    c(e)

            @block.vector
            def _(e):
                for c in S.streams["dve"]:
                    c(e)

            @block.gpsimd
            def _(e):
                for c in S.streams["pool"]:
                    c(e)

            @block.sync
            def _(e):
                for c in S.streams["sp"]:
                    c(e)
    return nc


_CACHE = {}


def _masks():
    m = np.zeros((5, 128, 512), np.float32)
    s = np.arange(128)[:, None]
    q = np.arange(128)[None, :]
    cur = (q >= s).astype(np.float32)
    prev = (s >= q).astype(np.float32)
    seq_eq = ((s // 8) == (q // 8)).astype(np.float32)
    curs = cur * seq_eq
    for j in range(4):
        m[0, :, j * 128:(j + 1) * 128] = cur
        m[1, :, j * 128:(j + 1) * 128] = prev
        m[3, :, j * 128:(j + 1) * 128] = curs
    t = (np.arange(512) % 8)[None, :]
    m[4] = (np.arange(128)[:, None] >= t).astype(np.float32)
    return m


def prep_inputs(x_prompt, x_sample, cache_k_win, cache_v_win, state_conv, state_ffn_conv, c_prompt, c_sample,
                w_mod, b_mod, g_mix, g_ffn, w_in, conv_w, sinks, w_out, w_gate_up, ffn_conv_w, w_down, g_final):
    f = lambda a: np.ascontiguousarray(np.asarray(a, dtype=np.float32))
    x_prompt = f(x_prompt); x_sample = f(x_sample)
    ck = f(cache_k_win)[0]; cv = f(cache_v_win)[0]; sc = f(state_conv)[0]; sf = f(state_ffn_conv)[0]
    c_prompt = f(c_prompt); c_sample = f(c_sample)
    shared = {
        "w_mod": f(w_mod)[0], "b_mod": f(b_mod)[0].reshape(6, D), "g_mix": f(g_mix).reshape(D), "g_ffn": f(g_ffn).reshape(D),
        "g_fin": f(g_final).reshape(D), "w_in": f(w_in)[0], "w_out": f(w_out)[0], "w_gu": f(w_gate_up)[0], "w_dn": f(w_down)[0],
        "cw": f(f(conv_w)[0].reshape(3, 16, 128).transpose(2, 1, 0)),
        "fcw": f(f(ffn_conv_w)[0].reshape(3, FC, 128).transpose(2, 1, 0)),
        "ident": np.eye(128, dtype=np.float32),
    }
    sk = f(sinks)[0]
    snk = np.zeros((128, 16), np.float32)
    for g in range(4):
        for jh in range(4):
            snk[0:64, 4 * g + jh] = sk[8 * g + 2 * jh]
            snk[64:128, 4 * g + jh] = sk[8 * g + 2 * jh + 1]
    shared["snk"] = snk
    sel = np.zeros((18, 2, 128), np.float32)
    sel[0, 0, :] = 1.0; sel[17, :, :] = 1.0
    for t in range(128):
        sel[1 + t // 8, 1, t] = 1.0
    shared["sel"] = sel
    mk = _masks()
    in_maps = []
    for i in range(8):
        n = i // 4; s = (i % 4) * 1024
        xpc = np.zeros((1280, D), np.float32)
        lo = s - 256
        if lo >= 0:
            xpc[:] = x_prompt[n, lo:lo + 1280]
        else:
            xpc[256:] = x_prompt[n, 0:1024]
        sq = slice(16 * i, 16 * i + 16)
        m = mk.copy()
        fl = 1.0 if s > 0 else 0.0
        m[2] = m[1] * fl
        d = dict(shared)
        d.update({
            "xp": xpc, "xs": f(x_sample[sq].reshape(128, D)),
            "c17": f(np.concatenate([c_prompt[n:n + 1], c_sample[sq]], axis=0)),
            "kcT": f(np.concatenate([ck[sq].transpose(3, 0, 2, 1)] * 2, axis=0)),
            "vc": f(cv[sq].reshape(16, 128, 256)), "kc": f(ck[sq].reshape(16, 128, 256)),
            "ush": f(sc[sq].reshape(16, 2, 16, 128).transpose(3, 2, 0, 1)),
            "gph": f(sf[sq].reshape(16, 2, FC, 128).transpose(3, 2, 0, 1)),
            "masks": f(m.transpose(1, 0, 2)), "flag": np.full((128, 1), fl, np.float32),
        })
        in_maps.append(d)
    return in_maps


def kernel(**inputs):
    in_maps = prep_inputs(**inputs)
    if "nc" not in _CACHE:
        _CACHE["nc"] = build_program()
    nc = _CACHE["nc"]
    res = run_bass_kernel_spmd(nc, in_maps, core_ids=list(range(8)))
    return assemble(res.results)


def assemble(R):
    y_prompt = np.zeros((2, 4096, D), np.float32)
    for i in range(8):
        y_prompt[i // 4, (i % 4) * 1024:(i % 4 + 1) * 1024] = R[i]["y_p"]
    y_sample = np.concatenate([R[i]["y_s"].reshape(16, 8, D) for i in range(8)], axis=0)
    kp = np.stack([R[3]["kwin_p"], R[7]["kwin_p"]]).reshape(1, 2, 128, 4, 64)
    vp = np.stack([R[3]["vwin_p"], R[7]["vwin_p"]]).reshape(1, 2, 128, 4, 64)
    cvp = np.stack([R[c]["conv_p"].transpose(2, 1, 0).reshape(2, 2048) for c in (3, 7)])[None]
    ffp = np.stack([R[c]["ffn_p"].transpose(2, 1, 0).reshape(2, DFF) for c in (3, 7)])[None]
    ks = np.concatenate([R[i]["kwin_s"] for i in range(8)], axis=0).reshape(1, 128, 128, 4, 64)
    vs = np.concatenate([R[i]["vwin_s"] for i in range(8)], axis=0).reshape(1, 128, 128, 4, 64)
    cvs = np.concatenate([R[i]["conv_s"].transpose(2, 3, 1, 0).reshape(16, 2, 2048) for i in range(8)], axis=0)[None]
    ffs = np.concatenate([R[i]["ffn_s"].transpose(2, 3, 1, 0).reshape(16, 2, DFF) for i in range(8)], axis=0)[None]
    return (y_prompt, y_sample, np.ascontiguousarray(kp), np.ascontiguousarray(vp), np.ascontiguousarray(cvp), np.ascontiguousarray(ffp),
            np.ascontiguousarray(ks), np.ascontiguousarray(vs), np.ascontiguousarray(cvs), np.ascontiguousarray(ffs))
```
